# Optimizing a Trainium2 kernel written in Bass

```python
import math
import jax, jax.numpy as jnp
from jax import lax
import numpy as np

D_MODEL = 2048
BATCH = 4
SEQ = 2048
DEPTH = 2

HEAD_DIM = 128
MIX_WIDTH = D_MODEL
N_HEADS_DIFF = MIX_WIDTH // 2 // HEAD_DIM
DIFF_QK_DIM = HEAD_DIM // 2
N_HEADS_SWA = MIX_WIDTH // 2 // HEAD_DIM
N_KV_SWA = N_HEADS_SWA // 4
WINDOW = 128
BLOCK = 128
D_FF = 4 * D_MODEL
ROPE_THETA = 10000.0
EPS = 1e-6
N_MOD = 6

DIFF_W = N_HEADS_DIFF * HEAD_DIM
SWA_Q_W = N_HEADS_SWA * HEAD_DIM
SWA_KV_W = N_KV_SWA * HEAD_DIM
IN_WIDTH = 3 * DIFF_W + SWA_Q_W + 2 * SWA_KV_W

kernel_name = "hymba_style_diffattn_swa_encoder"


def rmsnorm(x, g):
    xf = x.astype(jnp.float32)
    y = xf * lax.rsqrt(jnp.mean(xf * xf, axis=-1, keepdims=True) + EPS)
    return y.astype(x.dtype) * g


def rope_tables(positions, dim):
    inv = ROPE_THETA ** (-jnp.arange(0, dim, 2, dtype=jnp.float32) / dim)
    ang = positions.astype(jnp.float32)[..., None] * inv
    return jnp.cos(ang), jnp.sin(ang)


def apply_rope(x, cos, sin):
    shp = cos.shape[:2] + (1,) * (x.ndim - 3) + cos.shape[-1:]
    cos = cos.reshape(shp).astype(x.dtype)
    sin = sin.reshape(shp).astype(x.dtype)
    x1, x2 = jnp.split(x, 2, axis=-1)
    return jnp.concatenate([x1 * cos - x2 * sin, x2 * cos + x1 * sin], axis=-1)


def diff_attention(q, k, v, lam, lam_init, subln_g, cos, sin):
    b, s, h = q.shape[:3]
    nb = s // BLOCK
    q = apply_rope(q, cos, sin) * (DIFF_QK_DIM ** -0.5)
    k = apply_rope(k, cos, sin)
    qb = jnp.moveaxis(q.reshape(b, nb, BLOCK, h, 2, DIFF_QK_DIM), 1, 0)

    def one_block(qblk):
        logits = jnp.einsum('bqhmd,bkhmd->bhmqk', qblk, k).astype(jnp.float32)
        p = jax.nn.softmax(logits, axis=-1)
        w = (p[:, :, 0] - lam * p[:, :, 1]).astype(v.dtype)
        return jnp.einsum('bhqk,bkhd->bqhd', w, v)

    o = lax.map(one_block, qb)
    o = jnp.moveaxis(o, 0, 1).reshape(b, s, h, HEAD_DIM)
    o = rmsnorm(o, subln_g) * (1.0 - lam_init)
    return o.reshape(b, s, h * HEAD_DIM)


def window_gqa_sink(q, k, v, sink, cos, sin):
    b, s, h, d = q.shape
    g = k.shape[2]
    r = h // g
    nb = s // BLOCK
    q = apply_rope(q, cos, sin) * (d ** -0.5)
    k = apply_rope(k, cos, sin)
    qb = q.reshape(b, nb, BLOCK, g, r, d)

    def band(t):
        tp = jnp.pad(t, ((0, 0), (BLOCK, BLOCK), (0, 0), (0, 0))).reshape(b, nb + 2, BLOCK, g, d)
        return jnp.concatenate([tp[:, :-2], tp[:, 1:-1], tp[:, 2:]], axis=2)

    kb, vb = band(k), band(v)
    qi = jnp.arange(BLOCK)[:, None]
    kj = jnp.arange(3 * BLOCK)[None, :]
    rel = kj - BLOCK - qi
    kpos = jnp.arange(nb)[:, None, None] * BLOCK + kj[None] - BLOCK
    mask = (jnp.abs(rel)[None] <= WINDOW) & (kpos >= 0) & (kpos < s)

    logits = jnp.einsum('bnqgrd,bnkgd->bngrqk', qb, kb).astype(jnp.float32)
    logits = jnp.where(mask[None, :, None, None], logits, -jnp.inf)
    sink_l = sink.astype(jnp.float32).reshape(1, 1, g, r, 1, 1)
    m = jnp.maximum(logits.max(axis=-1, keepdims=True), sink_l)
    e = jnp.exp(logits - m)
    p = e / (e.sum(axis=-1, keepdims=True) + jnp.exp(sink_l - m))
    o = jnp.einsum('bngrqk,bnkgd->bnqgrd', p.astype(v.dtype), vb)
    return o.reshape(b, s, h * d)


def setup_inputs(seed: int = 0) -> dict:
    key = jax.random.key(seed)
    ks = jax.random.split(key, 16)
    nrm = jax.random.normal
    x = nrm(ks[0], (BATCH, SEQ, D_MODEL), jnp.float32)
    c = nrm(ks[1], (BATCH, D_MODEL), jnp.float32)
    offsets = jax.random.randint(ks[2], (BATCH, 1), 0, SEQ, dtype=jnp.int32)
    positions = (offsets + jnp.arange(SEQ, dtype=jnp.int32)[None, :]).astype(jnp.int32)
    ada_w = nrm(ks[3], (DEPTH, D_MODEL, N_MOD * D_MODEL), jnp.float32) * (0.5 * D_MODEL ** -0.5)
    ada_b = nrm(ks[4], (DEPTH, N_MOD * D_MODEL), jnp.float32) * 0.1
    norm_mix = 1.0 + 0.05 * nrm(ks[5], (DEPTH, D_MODEL), jnp.float32)
    w_in = nrm(ks[6], (DEPTH, D_MODEL, IN_WIDTH), jnp.float32) * (D_MODEL ** -0.5)
    diff_lambda = nrm(ks[7], (DEPTH, 4, DIFF_QK_DIM), jnp.float32) * 0.1
    diff_subln = 1.0 + 0.05 * nrm(ks[8], (DEPTH, HEAD_DIM), jnp.float32)
    swa_sink = nrm(ks[9], (DEPTH, N_HEADS_SWA), jnp.float32)
    w_out = nrm(ks[10], (DEPTH, MIX_WIDTH, D_MODEL), jnp.float32) * (MIX_WIDTH ** -0.5)
    norm_mlp = 1.0 + 0.05 * nrm(ks[11], (DEPTH, D_MODEL), jnp.float32)
    w_up = nrm(ks[12], (DEPTH, D_MODEL, D_FF), jnp.float32) * (D_MODEL ** -0.5)
    w_down = nrm(ks[13], (DEPTH, D_FF, D_MODEL), jnp.float32) * (D_FF ** -0.5)
    final_norm = 1.0 + 0.05 * nrm(ks[14], (D_MODEL,), jnp.float32)
    return {"x": x, "c": c, "positions": positions, "ada_w": ada_w, "ada_b": ada_b,
            "norm_mix": norm_mix, "w_in": w_in, "diff_lambda": diff_lambda,
            "diff_subln": diff_subln, "swa_sink": swa_sink, "w_out": w_out,
            "norm_mlp": norm_mlp, "w_up": w_up, "w_down": w_down, "final_norm": final_norm}


def reference(x, c, positions, ada_w, ada_b, norm_mix, w_in, diff_lambda, diff_subln,
              swa_sink, w_out, norm_mlp, w_up, w_down, final_norm):
    b, s, _ = x.shape
    cos_a, sin_a = rope_tables(positions, DIFF_QK_DIM)
    cos_b, sin_b = rope_tables(positions, HEAD_DIM)
    c_act = jax.nn.silu(c)
    splits = np.cumsum([DIFF_W, DIFF_W, DIFF_W, SWA_Q_W, SWA_KV_W]).tolist()

    for layer in range(DEPTH):
        mod = c_act @ ada_w[layer] + ada_b[layer]
        sh1, sc1, g1, sh2, sc2, g2 = [t[:, None, :] for t in jnp.split(mod, N_MOD, axis=-1)]

        h = rmsnorm(x, norm_mix[layer]) * (1.0 + sc1) + sh1
        proj = h @ w_in[layer]
        qa, ka, va, qb, kb, vb = jnp.split(proj, splits, axis=-1)
        qa = qa.reshape(b, s, N_HEADS_DIFF, 2, DIFF_QK_DIM)
        ka = ka.reshape(b, s, N_HEADS_DIFF, 2, DIFF_QK_DIM)
        va = va.reshape(b, s, N_HEADS_DIFF, HEAD_DIM)
        lam_init = 0.8 - 0.6 * math.exp(-0.3 * layer)
        lp = diff_lambda[layer].astype(jnp.float32)
        lam = jnp.exp(jnp.sum(lp[0] * lp[1])) - jnp.exp(jnp.sum(lp[2] * lp[3])) + lam_init
        out_a = diff_attention(qa, ka, va, lam, lam_init, diff_subln[layer], cos_a, sin_a)

        qb = qb.reshape(b, s, N_HEADS_SWA, HEAD_DIM)
        kb = kb.reshape(b, s, N_KV_SWA, HEAD_DIM)
        vb = vb.reshape(b, s, N_KV_SWA, HEAD_DIM)
        out_b = window_gqa_sink(qb, kb, vb, swa_sink[layer], cos_b, sin_b)

        mixed = jnp.concatenate([out_a, out_b], axis=-1) @ w_out[layer]
        x = x + g1 * mixed

        h2 = rmsnorm(x, norm_mlp[layer]) * (1.0 + sc2) + sh2
        x = x + g2 * (jnp.square(jax.nn.relu(h2 @ w_up[layer])) @ w_down[layer])

    return rmsnorm(x, final_norm)
```

```python
import math
from contextlib import ExitStack

import numpy as np
import concourse.bass as bass
import concourse.mybir as mybir
from concourse.bass_utils import run_bass_kernel_spmd

F32 = mybir.dt.float32
BF16 = mybir.dt.bfloat16
I32 = mybir.dt.int32
ALU = mybir.AluOpType
AF = mybir.ActivationFunctionType
AX = mybir.AxisListType

D = 2048
S_LEN = 2048
NB = 4
DEPTH = 2
DFF = 8192
INW = 4608
NT = 8
NEG = -30000.0
EPS = 1e-6
WCOLS = 256
NWBUF = 2
TWO_PI = 2.0 * math.pi
C1 = 6.28125
C2 = TWO_PI - C1
MAGIC = 12582912.0


class Sync:
    ENG = ("pe", "act", "dve", "pool", "sp")

    def __init__(self, nc, es):
        self.nc = nc
        self.es = es
        self.eng = {"pe": nc.tensor, "act": nc.scalar, "dve": nc.vector,
                    "pool": nc.gpsimd, "sp": nc.sync}
        self.sem = {}
        self.cnt = {}
        for k in self.ENG:
            self._mk(k)
        self.known = {e: {} for e in self.ENG}
        self.lastw = {}
        self.reads = {}

    def _mk(self, k):
        self.sem[k] = self.es.enter_context(self.nc.semaphore("s_" + "".join(ch for ch in str(k) if ch.isalnum())))
        self.cnt[k] = 0

    def _wait(self, e, deps):
        best = {}
        for sk, c in deps:
            if c > best.get(sk, 0):
                best[sk] = c
        for sk, c in best.items():
            if sk == e:
                if e == "pe":
                    continue
                if self.cnt[e] - c >= 3:
                    continue
            if not isinstance(sk, str) or sk not in self.ENG:
                c = max(c, self.cnt[sk])
            if self.known[e].get(sk, 0) >= c:
                continue
            self.eng[e].wait_ge(self.sem[sk], c)
            self.known[e][sk] = c

    def op(self, e, fn, r=(), w=(), signal=True, dma=None, dma_inc=16):
        deps = []
        for k in r:
            if k in self.lastw:
                deps.append(self.lastw[k])
        for k in w:
            if k in self.lastw:
                deps.append(self.lastw[k])
            rd = self.reads.get(k)
            if rd:
                deps.extend(rd.items())
        self._wait(e, deps)
        ins = fn()
        if dma is not None:
            sk = ("dma", dma)
            if sk not in self.sem:
                self._mk(sk)
            ins.then_inc(self.sem[sk], dma_inc)
            self.cnt[sk] += dma_inc
            c = self.cnt[sk]
        else:
            sk = e
            if signal:
                ins.then_inc(self.sem[sk], 1)
                self.cnt[sk] += 1
                c = self.cnt[sk]
            else:
                c = self.cnt[sk] + 1
        for k in r:
            d = self.reads.setdefault(k, {})
            if c > d.get(sk, 0):
                d[sk] = c
        for k in w:
            self.lastw[k] = (sk, c)
            self.reads[k] = {}
        return ins

    def barrier(self, exclude=()):
        deps = [(sk, c) for sk, c in self.cnt.items() if c > 0 and sk not in exclude]
        for e in self.ENG:
            self._wait(e, deps)
        self.lastw = {k: v for k, v in self.lastw.items() if v[0] in exclude}
        self.reads = {k: {s: c for s, c in d.items() if s in exclude} for k, d in self.reads.items()}


ARENA_WORDS = 53100
KV_ROWS = 2560
CH_ROWS = 512
NCH = KV_ROWS // CH_ROWS
PASSW = 7200


def _view(ar, off, shape, dt, p0=0, p1=128):
    nel = 1
    for s in shape:
        nel *= s
    words = nel if dt in (F32, I32) else (nel + 1) // 2
    assert off + words <= ARENA_WORDS, (off, words)
    ap = ar[p0:p1, off:off + words]
    if dt != F32:
        ap = ap.bitcast(dt)
    if len(shape) == 2:
        ap = ap.rearrange("p (a b) -> p a b", a=shape[0])
    elif len(shape) == 3:
        ap = ap.rearrange("p (a b c) -> p a b c", a=shape[0], b=shape[1])
    return ap, words


class Bump:
    def __init__(self, ar, base, limit):
        self.ar, self.base, self.off, self.limit = ar, base, base, limit

    def a(self, shape, dt, p0=0, p1=128):
        ap, words = _view(self.ar, self.off, shape, dt, p0, p1)
        self.off += (words + 7) // 8 * 8
        assert self.off <= self.limit, (self.off, self.limit)
        return ap

    def reset(self):
        self.off = self.base


def build_program(layer_consts, final_norm=True):
    nc = bass.Bass("TRN2", target_bir_lowering=False)
    nl = len(layer_consts)

    def din(name, shape, dt=F32):
        return nc.dram_tensor(name, list(shape), dt, kind="ExternalInput").ap()

    x_own = din("x_own", [NT * 128, D])
    pos_in = din("pos", [128, NT], I32)
    edge_in = din("edge", [128, 2])
    c_col_in = din("c_col", [128, 16])
    fin_g_in = din("final_g", [1, D])
    LP = []
    for li in range(nl):
        LP.append(dict(
            ada_w=din(f"ada_w{li}", [D, 6 * D]), ada_b=din(f"ada_b{li}", [1, 6 * D]),
            nmix=din(f"nmix{li}", [128, 16]), nmlp=din(f"nmlp{li}", [128, 16]),
            w_in=din(f"w_in{li}", [D, INW]), w_out=din(f"w_out{li}", [D, D]),
            w_up=din(f"w_up{li}", [D, DFF]), w_down=din(f"w_down{li}", [DFF, D]),
            lam=din(f"lam{li}", [1, 256]), subln=din(f"subln{li}", [1, 128]), sink=din(f"sink{li}", [1, 8]),
            kv_own=[nc.dram_tensor(f"kv_own{li}_{c}", [CH_ROWS, 1024], BF16).ap() for c in range(NCH)],
            kv_all=[nc.dram_tensor(f"kv_all{li}_{c}", [2 * CH_ROWS, 1024], BF16).ap() for c in range(NCH)],
        ))
    y_out = nc.dram_tensor("y", [NT * 128, D], F32, kind="ExternalOutput").ap()

    es = ExitStack()
    with es:
        S = Sync(nc, es)
        es.enter_context(nc.Block())
        arena = es.enter_context(nc.sbuf_tensor("arena", [128, ARENA_WORDS], F32))
        PS = [es.enter_context(nc.psum_tensor(f"ps{i}", [128, 512], F32)) for i in range(8)]
        V = nc.vector
        G = nc.gpsimd

        pb_ = Bump(arena, 0, ARENA_WORDS)
        Wb = [pb_.a([16, WCOLS], BF16) for _ in range(NWBUF)]
        cosA = pb_.a([NT, 32], F32)
        sinA = pb_.a([NT, 32], F32)
        cosB = pb_.a([NT, 64], F32)
        sinB = pb_.a([NT, 64], F32)
        mask3 = pb_.a([3, 384], F32)
        g1rep = pb_.a([D], F32)
        g2rep = pb_.a([D], F32)
        ident = pb_.a([128], F32)
        ones_row = pb_.a([128], F32, 0, 1)
        cact = pb_.a([16], BF16)
        colv = pb_.a([4, 16], F32)
        small = pb_.a([64], F32)
        gsub = pb_.a([128], F32)
        sinkr = pb_.a([8], F32)
        stat = pb_.a([64], F32)
        xs = pb_.a([NT, D], F32)
        HBASE = pb_.off
        MBASE = HBASE + 8192
        ZBASE = MBASE + 8192
        TBASE = ZBASE + PASSW
        TEND = TBASE + 1536
        assert TEND <= ARENA_WORDS, TEND
        RH = Bump(arena, HBASE, MBASE)
        RM = Bump(arena, MBASE, ZBASE)
        RZ = Bump(arena, ZBASE, TBASE)
        RT = Bump(arena, TBASE, TEND)
        hT = RH.a([16, NT * 128], BF16)
        mixT = RM.a([16, NT * 128], BF16)
        uT = mixT

        wstate = {"i": 0}

        def load_wblock(src_ap, k0, n0):
            i = wstate["i"]
            wstate["i"] += 1
            slot = i % NWBUF
            buf = Wb[slot]
            src = src_ap[k0 * 128:(k0 + 16) * 128, n0:n0 + WCOLS].rearrange("(c p) n -> p c n", p=128)
            S.op("pool", lambda: G.dma_start(out=buf, in_=src), w=[("W", slot)], dma=("w", slot))
            return buf, ("W", slot)

        class WStream:
            def __init__(self, specs):
                self.specs, self.issued, self.n = specs, [], 0

            def get(self, i):
                while self.n <= min(i + NWBUF - 1, len(self.specs) - 1):
                    self.issued.append(load_wblock(*self.specs[self.n]))
                    self.n += 1
                return self.issued[i]

        def setup_constants():
            S.op("dve", lambda: V.memset(small[:, 8:9], EPS), w=["eps"])
            S.op("pool", lambda: G.memset(ident, 0.0), w=["ident"])
            S.op("pool", lambda: G.affine_select(
                out=ident, in_=ident, pattern=[[-1, 128]], compare_op=ALU.not_equal,
                fill=1.0, base=0, channel_multiplier=1), r=["ident"], w=["ident"])
            S.op("pool", lambda: G.memset(ones_row, 1.0), w=["ones_row"])
            S.op("pool", lambda: G.memset(mask3, 0.0), w=["mask3"])
            for v in range(3):
                S.op("pool", lambda v=v: G.affine_select(
                    out=mask3[:, v, 0:128], in_=mask3[:, v, 0:128], pattern=[[1, 128]],
                    compare_op=ALU.is_ge, fill=NEG, base=0, channel_multiplier=-1),
                    r=["mask3"], w=["mask3"])
                S.op("pool", lambda v=v: G.affine_select(
                    out=mask3[:, v, 256:384], in_=mask3[:, v, 256:384], pattern=[[-1, 128]],
                    compare_op=ALU.is_ge, fill=NEG, base=0, channel_multiplier=1),
                    r=["mask3"], w=["mask3"])
            RZ.reset()
            edge = RZ.a([2], F32)
            posi = RZ.a([NT], I32)
            posf = RZ.a([NT], F32)
            iot = RZ.a([64], F32)
            inv = RZ.a([64], F32)
            ang = RZ.a([NT, 64], F32)
            kk = RZ.a([NT, 64], F32)
            yy = RZ.a([NT, 64], F32)
            ccol = RZ.a([16], F32)
            S.op("sp", lambda: nc.sync.dma_start(out=edge, in_=edge_in), w=["edge"], dma="setup")
            S.op("sp", lambda: nc.sync.dma_start(out=posi, in_=pos_in), w=["posi"], dma="setup")
            S.op("sp", lambda: nc.sync.dma_start(out=ccol, in_=c_col_in), w=["ccol"], dma="setup")
            S.op("act", lambda: nc.scalar.activation(out=cact, in_=ccol, func=AF.Silu),
                 r=["ccol"], w=["cact"])
            S.op("dve", lambda: V.tensor_scalar(
                out=mask3[:, 1, 0:128], in0=mask3[:, 1, 0:128], scalar1=edge[:, 0:1],
                scalar2=None, op0=ALU.add), r=["mask3", "edge"], w=["mask3"])
            S.op("dve", lambda: V.tensor_scalar(
                out=mask3[:, 2, 256:384], in0=mask3[:, 2, 256:384], scalar1=edge[:, 1:2],
                scalar2=None, op0=ALU.add), r=["mask3", "edge"], w=["mask3"])
            S.op("dve", lambda: V.tensor_copy(out=posf, in_=posi), r=["posi"], w=["posf"])
            S.op("pool", lambda: G.iota(iot, pattern=[[1, 64]], base=0, channel_multiplier=0,
                                        allow_small_or_imprecise_dtypes=True), w=["iot"])
            for (dim, ct, st) in ((64, cosA, sinA), (128, cosB, sinB)):
                hd = dim // 2
                S.op("act", lambda hd=hd, dim=dim: nc.scalar.activation(
                    out=inv[:, 0:hd], in_=iot[:, 0:hd], func=AF.Exp,
                    scale=-(2.0 / dim) * math.log(10000.0)), r=["iot"], w=["inv"])
                for t in range(NT):
                    S.op("dve", lambda t=t, hd=hd: V.tensor_scalar(
                        out=ang[:, t, 0:hd], in0=inv[:, 0:hd], scalar1=posf[:, t:t + 1],
                        scalar2=None, op0=ALU.mult), r=["inv", "posf"], w=["ang"])
                a3 = ang[:, :, 0:hd]
                k3 = kk[:, :, 0:hd]
                y3 = yy[:, :, 0:hd]
                for which in range(2):
                    dst = st if which == 0 else ct
                    if which == 1:
                        S.op("dve", lambda a3=a3: V.tensor_scalar(
                            out=a3, in0=a3, scalar1=0.5 * math.pi, scalar2=None, op0=ALU.add),
                            r=["ang"], w=["ang"])
                    S.op("dve", lambda a3=a3, k3=k3: V.tensor_scalar(
                        out=k3, in0=a3, scalar1=1.0 / TWO_PI, scalar2=MAGIC,
                        op0=ALU.mult, op1=ALU.add), r=["ang"], w=["kk"])
                    S.op("dve", lambda k3=k3: V.tensor_scalar(
                        out=k3, in0=k3, scalar1=-MAGIC, scalar2=None, op0=ALU.add),
                        r=["kk"], w=["kk"])
                    S.op("dve", lambda a3=a3, k3=k3, y3=y3: V.scalar_tensor_tensor(
                        out=y3, in0=k3, scalar=-C1, in1=a3, op0=ALU.mult, op1=ALU.add),
                        r=["kk", "ang"], w=["yy"])
                    S.op("dve", lambda k3=k3, y3=y3: V.scalar_tensor_tensor(
                        out=y3, in0=k3, scalar=-C2, in1=y3, op0=ALU.mult, op1=ALU.add),
                        r=["kk", "yy"], w=["yy"])
                    S.op("dve", lambda y3=y3: V.tensor_scalar(
                        out=y3, in0=y3, scalar1=3.1415925, scalar2=-3.1415925,
                        op0=ALU.min, op1=ALU.max), r=["yy"], w=["yy"])
                    S.op("act", lambda y3=y3, dst=dst: nc.scalar.activation(
                        out=dst, in_=y3, func=AF.Sin), r=["yy"], w=[("tab", dim)])
            S.barrier()

        def layer_params(P, lam_init):
            RZ.reset()
            lamt = RZ.a([256], F32)
            lprod = RZ.a([128], F32)
            nm = RZ.a([2, 16], F32)
            rowbuf = RZ.a([D], F32, 0, 1)
            abrow = RZ.a([D], F32, 0, 1)
            S.op("sp", lambda: nc.sync.dma_start(out=lamt, in_=P["lam"][0].partition_broadcast(128)),
                 w=["lamt"], dma="setup")
            S.op("sp", lambda: nc.sync.dma_start(out=gsub, in_=P["subln"][0].partition_broadcast(128)),
                 w=["gsub"], dma="setup")
            S.op("sp", lambda: nc.sync.dma_start(out=sinkr, in_=P["sink"][0].partition_broadcast(128)),
                 w=["sinkr"], dma="setup")
            S.op("sp", lambda: nc.sync.dma_start(out=nm[:, 0, :], in_=P["nmix"]), w=["nm"], dma="setup")
            S.op("sp", lambda: nc.sync.dma_start(out=nm[:, 1, :], in_=P["nmlp"]), w=["nm"], dma="setup")
            S.op("dve", lambda: V.tensor_tensor(out=lprod[:, 0:64], in0=lamt[:, 0:64],
                                                in1=lamt[:, 64:128], op=ALU.mult),
                 r=["lamt"], w=["lprod"])
            S.op("dve", lambda: V.tensor_tensor(out=lprod[:, 64:128], in0=lamt[:, 128:192],
                                                in1=lamt[:, 192:256], op=ALU.mult),
                 r=["lamt"], w=["lprod"])
            S.op("dve", lambda: V.tensor_reduce(out=small[:, 2:4],
                                                in_=lprod.rearrange("p (a b) -> p a b", a=2),
                                                axis=AX.X, op=ALU.add),
                 r=["lprod"], w=["small"])
            S.op("act", lambda: nc.scalar.activation(out=small[:, 4:6], in_=small[:, 2:4], func=AF.Exp),
                 r=["small"], w=["small"])
            S.op("dve", lambda: V.tensor_tensor(out=small[:, 0:1], in0=small[:, 4:5],
                                                in1=small[:, 5:6], op=ALU.subtract),
                 r=["small"], w=["small"])
            S.op("dve", lambda: V.tensor_scalar(out=small[:, 0:1], in0=small[:, 0:1],
                                                scalar1=float(lam_init), scalar2=None, op0=ALU.add),
                 r=["small"], w=["small"])
            S.op("dve", lambda: V.tensor_scalar(out=small[:, 1:2], in0=small[:, 0:1],
                                                scalar1=-1.0, scalar2=None, op0=ALU.mult),
                 r=["small"], w=["small"])
            S.op("dve", lambda: V.tensor_scalar(out=gsub, in0=gsub,
                                                scalar1=float(1.0 - lam_init), scalar2=None,
                                                op0=ALU.mult), r=["gsub"], w=["gsub"])
            ws = WStream([(P["ada_w"], 0, n * WCOLS) for n in range(48)])
            for v in range(6):
                S.op("sp", lambda v=v: nc.sync.dma_start(
                    out=abrow, in_=P["ada_b"][0:1, v * D:(v + 1) * D]),
                    w=["abrow"], dma="setup")
                for j in range(8):
                    bi = v * 8 + j
                    wbuf, wkey = ws.get(bi)
                    pb = PS[bi % 2]
                    for k in range(16):
                        S.op("pe", lambda k=k, wbuf=wbuf, pb=pb: nc.tensor.matmul(
                            pb[0:1, 0:WCOLS], lhsT=cact[:, k:k + 1], rhs=wbuf[:, k, :],
                            start=(k == 0), stop=(k == 15)),
                            r=[wkey, "cact"], w=[("ps", bi % 2)], signal=(k == 15))
                    S.op("dve", lambda j=j, pb=pb: V.tensor_tensor(
                        out=rowbuf[0:1, j * WCOLS:(j + 1) * WCOLS], in0=pb[0:1, 0:WCOLS],
                        in1=abrow[0:1, j * WCOLS:(j + 1) * WCOLS], op=ALU.add),
                        r=[("ps", bi % 2), "abrow"], w=["rowbuf"])
                if v in (2, 5):
                    dst = g1rep if v == 2 else g2rep
                    for j in range(4):
                        pb = PS[2 + (j % 2)]
                        S.op("pe", lambda j=j, pb=pb: nc.tensor.matmul(
                            pb[:, :], lhsT=ones_row[0:1, :], rhs=rowbuf[0:1, j * 512:(j + 1) * 512],
                            start=True, stop=True), r=["rowbuf", "ones_row"], w=[("ps", 2 + j % 2)])
                        S.op("act", lambda j=j, pb=pb, dst=dst: nc.scalar.copy(
                            out=dst[:, j * 512:(j + 1) * 512], in_=pb[:, :]),
                            r=[("ps", 2 + j % 2)], w=[("grep", v)])
                else:
                    pb = PS[4]
                    for c in range(16):
                        S.op("pe", lambda c=c, pb=pb: nc.tensor.matmul(
                            pb[:, c:c + 1], lhsT=rowbuf[0:1, c * 128:(c + 1) * 128],
                            rhs=ones_row[0:1, 0:1], start=True, stop=True),
                            r=["rowbuf", "ones_row"], w=[("ps", 4)], signal=(c == 15))
                    half = 0 if v < 3 else 1
                    if v in (0, 3):
                        S.op("dve", lambda pb=pb, half=half: V.tensor_copy(
                            out=colv[:, 2 * half + 1, :], in_=pb[:, 0:16]),
                            r=[("ps", 4)], w=["colv"])
                    else:
                        S.op("dve", lambda pb=pb, half=half: V.scalar_tensor_tensor(
                            out=colv[:, 2 * half, :], in0=pb[:, 0:16], scalar=1.0,
                            in1=nm[:, half, :], op0=ALU.add, op1=ALU.mult),
                            r=[("ps", 4), "nm"], w=["colv"])
            S.barrier()

        def norm_to_hT(which, stat_off):
            RZ.reset()
            xh = [RZ.a([D], F32) for _ in range(2)]
            RT.reset()
            junk = RT.a([D], BF16)
            for t in range(NT):
                xa, xk = xs[:, t, :], ("x", t)
                so = stat_off + t
                S.op("act", lambda xa=xa, so=so: nc.scalar.activation(
                    out=junk, in_=xa, func=AF.Square, accum_out=stat[:, so:so + 1]),
                    r=[xk], w=["junk", ("stat", so)])
                S.op("act", lambda so=so: nc.scalar.activation(
                    out=stat[:, so:so + 1], in_=stat[:, so:so + 1], func=AF.Sqrt,
                    scale=1.0 / D, bias=small[:, 8:9]), r=[("stat", so), "eps"], w=[("stat", so)])
                S.op("dve", lambda so=so: V.reciprocal(out=stat[:, so:so + 1], in_=stat[:, so:so + 1]),
                     r=[("stat", so)], w=[("stat", so)])
                xb = xh[t % 2]
                S.op("dve", lambda xa=xa, xb=xb, so=so: V.tensor_scalar(
                    out=xb, in0=xa, scalar1=stat[:, so:so + 1], scalar2=None, op0=ALU.mult),
                    r=[xk, ("stat", so)], w=[("xh", t % 2)])
                for g4 in range(4):
                    pi = g4
                    pb = PS[pi]
                    for i in range(4):
                        c = g4 * 4 + i
                        S.op("pe", lambda c=c, i=i, pb=pb, xb=xb: nc.tensor.transpose(
                            pb[:, i * 128:(i + 1) * 128], xb[:, c * 128:(c + 1) * 128], ident),
                            r=[("xh", t % 2), "ident"], w=[("ps", pi)], signal=(i == 3))
                    for i in range(4):
                        c = g4 * 4 + i
                        dst = hT[:, c, t * 128:(t + 1) * 128]
                        src = pb[:, i * 128:(i + 1) * 128]
                        if i % 2 == 0:
                            S.op("act", lambda dst=dst, src=src, c=c: nc.scalar.activation(
                                out=dst, in_=src, func=AF.Identity,
                                scale=colv[:, 2 * which, c:c + 1], bias=colv[:, 2 * which + 1, c:c + 1]),
                                r=[("ps", pi), "colv"], w=[("hT", t)])
                        else:
                            S.op("dve", lambda dst=dst, src=src, c=c: V.tensor_scalar(
                                out=dst, in0=src, scalar1=colv[:, 2 * which, c:c + 1],
                                scalar2=colv[:, 2 * which + 1, c:c + 1], op0=ALU.mult, op1=ALU.add),
                                r=[("ps", pi), "colv"], w=[("hT", t)])
            S.barrier()

        pstate = {"n": 0}

        def proj_block(wbuf, wkey):
            for t in range(NT):
                n = pstate["n"]
                pstate["n"] += 1
                pi = n % 2
                pb = PS[pi]
                for k in range(16):
                    S.op("pe", lambda k=k, t=t, pb=pb: nc.tensor.matmul(
                        pb[:, 0:WCOLS], lhsT=hT[:, k, t * 128:(t + 1) * 128], rhs=wbuf[:, k, :],
                        start=(k == 0), stop=(k == 15)),
                        r=[wkey, ("hT", t)], w=[("ps", pi)], signal=(k == 15))
                flush_rope()
                yield n, t, pb[:, 0:WCOLS], ("ps", pi)
            flush_rope()

        def rope_to_T(ps_ap, pskey, slot, dim, scale, cos_t, sin_t, dsts, ropebuf, n):
            b = n % 2
            xs_, t1, t2 = ropebuf[b]
            ng = WCOLS // dim
            hd = dim // 2
            S.op("act", lambda: nc.scalar.activation(out=xs_, in_=ps_ap, func=AF.Copy, scale=float(scale)),
                 r=[pskey], w=[("ropeX", b)])
            x4 = xs_.rearrange("p (g h d) -> p g h d", g=ng, h=2)
            a4 = t1.rearrange("p (g h d) -> p g h d", g=ng, h=2)
            b4 = t2.rearrange("p (g h d) -> p g h d", g=ng, h=2)
            cb = cos_t[:, slot, :].unsqueeze(1).unsqueeze(1).to_broadcast([128, ng, 2, hd])
            sb1 = sin_t[:, slot, :].unsqueeze(1).to_broadcast([128, ng, hd])
            S.op("dve", lambda: V.tensor_tensor(out=a4, in0=x4, in1=cb, op=ALU.mult),
                 r=[("ropeX", b), ("tab", dim)], w=[("ropeA", b)])
            S.op("pool", lambda: G.tensor_tensor(out=b4[:, :, 0, :], in0=x4[:, :, 1, :], in1=sb1, op=ALU.mult),
                 r=[("ropeX", b), ("tab", dim)], w=[("ropeB", b)])
            S.op("pool", lambda: G.tensor_tensor(out=b4[:, :, 1, :], in0=x4[:, :, 0, :], in1=sb1, op=ALU.mult),
                 r=[("ropeX", b), ("tab", dim)], w=[("ropeB", b)])
            S.op("dve", lambda: V.tensor_tensor(out=a4[:, :, 0, :], in0=a4[:, :, 0, :], in1=b4[:, :, 0, :],
                                                op=ALU.subtract),
                 r=[("ropeA", b), ("ropeB", b)], w=[("ropeA", b)])
            S.op("dve", lambda: V.tensor_tensor(out=a4[:, :, 1, :], in0=a4[:, :, 1, :], in1=b4[:, :, 1, :],
                                                op=ALU.add),
                 r=[("ropeA", b), ("ropeB", b)], w=[("ropeA", b)])
            pi = 2 + b
            pb = PS[pi]

            def tail():
                for j in range(2):
                    S.op("pe", lambda j=j: nc.tensor.transpose(
                        pb[:, j * 128:(j + 1) * 128], t1[:, j * 128:(j + 1) * 128], ident),
                        r=[("ropeA", b), "ident"], w=[("ps", pi)], signal=(j == 1))
                for j in range(2):
                    dst, dk = dsts[j]
                    if j == 0:
                        S.op("act", lambda dst=dst: nc.scalar.copy(out=dst, in_=pb[:, 0:128]),
                             r=[("ps", pi)], w=[dk])
                    else:
                        S.op("dve", lambda dst=dst: V.tensor_copy(out=dst, in_=pb[:, 128:256]),
                             r=[("ps", pi)], w=[dk])
            rope_tail.append(tail)

        rope_tail = []

        def flush_rope(keep=0):
            while len(rope_tail) > keep:
                rope_tail.pop(0)()

        def make_ropebuf():
            RT.reset()
            return [tuple(RT.a([WCOLS], F32) for _ in range(3)) for _ in range(2)]

        def layer(li, lam_init):
            P = LP[li]
            kv_own, kv_all = P["kv_own"], P["kv_all"]
            ccsems = [("dma", ("cc", li, c)) for c in range(NCH)]

            def allgather(c):
                S.op("pool", lambda: G.collective_compute(
                    "AllGather", op=ALU.bypass, replica_groups=[[0, 1], [2, 3], [4, 5], [6, 7]],
                    ins=[kv_own[c].opt()], outs=[kv_all[c].opt()]),
                    r=[("kv_own", c)], w=[("kv_all", c)], dma=("cc", li, c), dma_inc=1)
            layer_params(P, lam_init)
            norm_to_hT(0, 0)
            ropebuf = make_ropebuf()

            RZ.reset()
            KbT = RZ.a([2, NT * 128], BF16)
            Vb = RZ.a([NT, 256], BF16)
            zmark = RZ.off
            KTs = [RZ.a([2, NT * 128], BF16) for _ in range(2)]
            Vs = [RZ.a([NT, 256], BF16) for _ in range(2)]
            ws = WStream([(P["w_in"], 0, 1024 + 256 * j) for j in range(4)] +
                         [(P["w_in"], 0, 2048 + 256 * j) for j in range(4)] +
                         [(P["w_in"], 0, 4096), (P["w_in"], 0, 4352)] +
                         [(P["w_in"], 0, 3072 + 256 * j) for j in range(4)])
            for j in range(4):
                wbuf, wkey = ws.get(j)
                st = KTs[j % 2]
                for n, t, pa, pk in proj_block(wbuf, wkey):
                    dsts = [(st[:, q, t * 128:(t + 1) * 128], ("KTs", j % 2)) for q in range(2)]
                    rope_to_T(pa, pk, t, 64, 1.0, cosA, sinA, dsts, ropebuf, n)
                S.op("sp", lambda j=j, st=st: nc.sync.dma_start(
                    out=kv_own[j // 2][(j % 2) * 256:(j % 2) * 256 + 256, :].rearrange("(q p) c -> p q c", p=128),
                    in_=st), r=[("KTs", j % 2)], w=[("kv_own", j // 2)], dma=("kvs", j % 2))
                if j % 2 == 1:
                    allgather(j // 2)
            for j in range(4):
                wbuf, wkey = ws.get(4 + j)
                st = Vs[j % 2]
                for n, t, pa, pk in proj_block(wbuf, wkey):
                    S.op("act", lambda pa=pa, t=t, st=st: nc.scalar.copy(out=st[:, t, :], in_=pa),
                         r=[pk], w=[("Vs", j % 2)])
                for hh in range(2):
                    S.op("sp", lambda j=j, st=st, hh=hh: nc.sync.dma_start(
                        out=kv_own[2 + hh][:, j * 256:(j + 1) * 256].rearrange("(t p) c -> p t c", p=128),
                        in_=st[:, hh * 4:(hh + 1) * 4, :]), r=[("Vs", j % 2)], w=[("kv_own", 2 + hh)],
                        dma=("kvs", 2 + j % 2))
            allgather(2)
            allgather(3)
            wbuf, wkey = ws.get(8)
            for n, t, pa, pk in proj_block(wbuf, wkey):
                dsts = [(KbT[:, g, t * 128:(t + 1) * 128], ("KbT", t)) for g in range(2)]
                rope_to_T(pa, pk, t, 128, 1.0, cosB, sinB, dsts, ropebuf, n)
            S.op("sp", lambda: nc.sync.dma_start(
                out=kv_own[4][0:256, :].rearrange("(g p) c -> p g c", p=128), in_=KbT),
                r=[("KbT", t) for t in range(NT)], w=[("kv_own", 4)], dma=("kvs", 0))
            wbuf, wkey = ws.get(9)
            for n, t, pa, pk in proj_block(wbuf, wkey):
                S.op("act", lambda pa=pa, t=t: nc.scalar.copy(out=Vb[:, t, :], in_=pa), r=[pk], w=[("Vb", t)])
            S.op("sp", lambda: nc.sync.dma_start(
                out=kv_own[4][256:512, :].rearrange("a (f c) -> (a f) c", f=4).rearrange(
                    "(t p) c -> p t c", p=128), in_=Vb),
                r=[("Vb", t) for t in range(NT)], w=[("kv_own", 4)], dma=("kvs", 1))
            allgather(4)

            RZ.off = zmark
            QbT = RZ.a([4, NT * 128], BF16)
            Sm = [RZ.a([384], F32) for _ in range(2)]
            Ee = [RZ.a([384], F32) for _ in range(2)]
            ET = [RZ.a([3, 128], BF16) for _ in range(2)]
            ob = [RZ.a([128], F32) for _ in range(2)]
            sst = RZ.a([4, 8], F32)
            KbTd = RZ.a([2, 256], BF16)
            Vbd = RZ.a([2, 256], BF16)
            S.barrier(exclude=ccsems)

            def vb_sec(r):
                return kv_all[4][r * CH_ROWS + 256:r * CH_ROWS + 512, :].rearrange("a (f c) -> (a f) c", f=4)

            S.op("sp", lambda: nc.sync.dma_start(
                out=KbTd[:, :, 0:128],
                in_=kv_all[4][0:256, 896:1024].rearrange("(g p) c -> p g c", p=128)),
                r=[("kv_all", 4)], w=["KbTd"], dma="bd")
            S.op("sp", lambda: nc.sync.dma_start(
                out=KbTd[:, :, 128:256],
                in_=kv_all[4][CH_ROWS:CH_ROWS + 256, 0:128].rearrange("(g p) c -> p g c", p=128)),
                r=[("kv_all", 4)], w=["KbTd"], dma="bd")
            S.op("sp", lambda: nc.sync.dma_start(out=Vbd[:, 0, :], in_=vb_sec(0)[896:1024, :]),
                 r=[("kv_all", 4)], w=["Vbd"], dma="bd")
            S.op("sp", lambda: nc.sync.dma_start(out=Vbd[:, 1, :], in_=vb_sec(1)[0:128, :]),
                 r=[("kv_all", 4)], w=["Vbd"], dma="bd")

            def kb_src(sl, g):
                if sl == "prev":
                    return KbTd[:, g, 0:128], "KbTd"
                if sl == "next":
                    return KbTd[:, g, 128:256], "KbTd"
                return KbT[:, g, sl * 128:(sl + 1) * 128], ("KbT", sl)

            def vb_src(sl, g):
                if sl == "prev":
                    return Vbd[:, 0, g * 128:(g + 1) * 128], "Vbd"
                if sl == "next":
                    return Vbd[:, 1, g * 128:(g + 1) * 128], "Vbd"
                return Vb[:, sl, g * 128:(g + 1) * 128], ("Vb", sl)

            for g in range(2):
                for j in range(2):
                    wbuf, wkey = ws.get(10 + 2 * g + j)
                    for n, t, pa, pk in proj_block(wbuf, wkey):
                        dsts = [(QbT[:, 2 * j + q, t * 128:(t + 1) * 128], ("QbT", t)) for q in range(2)]
                        rope_to_T(pa, pk, t, 128, 128.0 ** -0.5, cosB, sinB, dsts, ropebuf, n)
                units = [(i, hl) for i in range(NT) for hl in range(4)]

                def stage_a(u, g=g):
                    i, hl = units[u]
                    b = u % 2
                    slots = ((i - 1) if i > 0 else "prev", i, (i + 1) if i < NT - 1 else "next")
                    pS = PS[4 + b]
                    for bi_, sl in enumerate(slots):
                        ksrc, kkey = kb_src(sl, g)
                        S.op("pe", lambda bi_=bi_, ksrc=ksrc, pS=pS, hl=hl, i=i: nc.tensor.matmul(
                            pS[:, bi_ * 128:(bi_ + 1) * 128], lhsT=QbT[:, hl, i * 128:(i + 1) * 128],
                            rhs=ksrc, start=True, stop=True),
                            r=[("QbT", i), kkey], w=[("ps", 4 + b)], signal=(bi_ == 2))

                def stage_b1(u, g=g):
                    i, hl = units[u]
                    b = u % 2
                    sb4 = u % 4
                    h = 4 * g + hl
                    mv = 1 if i == 0 else (2 if i == NT - 1 else 0)
                    pS = PS[4 + b]
                    S.op("dve", lambda: V.tensor_tensor(
                        out=Sm[b], in0=pS[:, 0:384], in1=mask3[:, mv, :], op=ALU.add),
                        r=[("ps", 4 + b), "mask3"], w=[("Sm", b)])
                    S.op("dve", lambda: V.tensor_reduce(
                        out=sst[:, sb4, 0:1], in_=Sm[b], axis=AX.X, op=ALU.max),
                        r=[("Sm", b)], w=[("sst", sb4)])
                    S.op("dve", lambda: V.tensor_scalar(
                        out=sst[:, sb4, 1:2], in0=sst[:, sb4, 0:1], scalar1=sinkr[:, h:h + 1],
                        scalar2=-1.0, op0=ALU.max, op1=ALU.mult),
                        r=[("sst", sb4), "sinkr"], w=[("sst1", sb4)])

                def stage_b2(u, g=g):
                    i, hl = units[u]
                    b = u % 2
                    sb4 = u % 4
                    h = 4 * g + hl
                    S.op("act", lambda: nc.scalar.activation(
                        out=Ee[b], in_=Sm[b], func=AF.Exp, bias=sst[:, sb4, 1:2],
                        accum_out=sst[:, sb4, 2:3]),
                        r=[("Sm", b), ("sst1", sb4)], w=[("Ee", b), ("sst2", sb4)])
                    S.op("act", lambda: nc.scalar.activation(
                        out=sst[:, sb4, 3:4], in_=sinkr[:, h:h + 1], func=AF.Exp, bias=sst[:, sb4, 1:2]),
                        r=[("sst1", sb4), "sinkr"], w=[("sst3", sb4)])
                    S.op("dve", lambda: V.tensor_tensor(
                        out=sst[:, sb4, 4:5], in0=sst[:, sb4, 2:3], in1=sst[:, sb4, 3:4], op=ALU.add),
                        r=[("sst2", sb4), ("sst3", sb4)], w=[("sst4", sb4)])
                    S.op("dve", lambda: V.reciprocal(out=sst[:, sb4, 5:6], in_=sst[:, sb4, 4:5]),
                         r=[("sst4", sb4)], w=[("sst5", sb4)])

                def stage_c(u, g=g):
                    i, hl = units[u]
                    b = u % 2
                    slots = ((i - 1) if i > 0 else "prev", i, (i + 1) if i < NT - 1 else "next")
                    pT = PS[6 + b]
                    for bi_ in range(3):
                        S.op("pe", lambda bi_=bi_: nc.tensor.transpose(
                            pT[:, bi_ * 128:(bi_ + 1) * 128], Ee[b][:, bi_ * 128:(bi_ + 1) * 128],
                            ident), r=[("Ee", b), "ident"], w=[("ps", 6 + b)], signal=(bi_ == 2))
                    S.op("act", lambda: nc.scalar.copy(
                        out=ET[b].rearrange("p a b -> p (a b)"), in_=pT[:, 0:384]),
                        r=[("ps", 6 + b)], w=[("ET", b)])

                def stage_d(u, g=g):
                    i, hl = units[u]
                    b = u % 2
                    h = 4 * g + hl
                    slots = ((i - 1) if i > 0 else "prev", i, (i + 1) if i < NT - 1 else "next")
                    pO = PS[2 + b]
                    for bi_, sl in enumerate(slots):
                        vsrc, vkey = vb_src(sl, g)
                        S.op("pe", lambda bi_=bi_, vsrc=vsrc: nc.tensor.matmul(
                            pO[:, 0:128], lhsT=ET[b][:, bi_, :], rhs=vsrc,
                            start=(bi_ == 0), stop=(bi_ == 2)),
                            r=[("ET", b), vkey], w=[("ps", 2 + b)], signal=(bi_ == 2))
                    S.op("act", lambda: nc.scalar.activation(
                        out=ob[b], in_=pO[:, 0:128], func=AF.Identity, scale=sst[:, u % 4, 5:6]),
                        r=[("ps", 2 + b), ("sst5", u % 4)], w=[("ob", b)])

                def stage_e(u, g=g):
                    i, hl = units[u]
                    b = u % 2
                    h = 4 * g + hl
                    pM = PS[b]
                    S.op("pe", lambda: nc.tensor.transpose(pM[:, 0:128], ob[b], ident),
                         r=[("ob", b), "ident"], w=[("ps", b)])
                    S.op("dve", lambda: V.tensor_copy(
                        out=mixT[:, 8 + h, i * 128:(i + 1) * 128], in_=pM[:, 0:128]),
                        r=[("ps", b)], w=[("mixT", i)])

                nu = len(units)
                for step in range(nu + 5):
                    if step < nu:
                        stage_a(step)
                    if 1 <= step <= nu:
                        stage_b1(step - 1)
                    if 2 <= step <= nu + 1:
                        stage_b2(step - 2)
                    if 3 <= step <= nu + 2:
                        stage_c(step - 3)
                    if 4 <= step <= nu + 3:
                        stage_d(step - 4)
                    if 5 <= step <= nu + 4:
                        stage_e(step - 5)
            S.barrier()

            for pa_i in range(4):
                RZ.reset()
                KT = RZ.a([2, 16 * 128], BF16)
                Va = RZ.a([2, 16, 132], BF16)
                QT = RZ.a([2, NT * 128], BF16)
                PT = [RZ.a([512], BF16) for _ in range(3)]
                o0 = RZ.a([4, 132], F32)
                osb = [RZ.a([128], F32) for _ in range(4)]
                dst_ = RZ.a([4, 8], F32)
                junk2 = RZ.a([128], F32)
                S.op("pool", lambda: G.memset(Va[:, :, :, 128:129], 1.0), w=["Va1"])
                for r_ in range(2):
                    for hl in range(2):
                        hg = 2 * pa_i + hl
                        S.op("sp", lambda r_=r_, hl=hl, hg=hg: nc.sync.dma_start(
                            out=KT[:, hl, r_ * 1024:(r_ + 1) * 1024],
                            in_=kv_all[hg // 4][r_ * CH_ROWS + (hg % 4) * 128:r_ * CH_ROWS + (hg % 4 + 1) * 128, :]),
                            r=[("kv_all", hg // 4)], w=[("KT", r_)], dma=("kvl", r_))
                    for hl in range(2):
                        hg = 2 * pa_i + hl
                        for hh in range(2):
                            S.op("sp", lambda r_=r_, hl=hl, hg=hg, hh=hh: nc.sync.dma_start(
                                out=Va[:, hl, r_ * 8 + hh * 4:r_ * 8 + hh * 4 + 4, 0:128],
                                in_=kv_all[2 + hh][r_ * CH_ROWS:(r_ + 1) * CH_ROWS,
                                                   hg * 128:(hg + 1) * 128].rearrange("(t p) d -> p t d", p=128)),
                                r=[("kv_all", 2 + hh)], w=[("Va", r_)], dma=("kvl", 2 + r_))
                ws = WStream([(P["w_in"], 0, 256 * pa_i)])
                wbuf, wkey = ws.get(0)
                for n, t, pa, pk in proj_block(wbuf, wkey):
                    dsts = [(QT[:, q, t * 128:(t + 1) * 128], ("QT", t)) for q in range(2)]
                    rope_to_T(pa, pk, t, 64, 0.125, cosA, sinA, dsts, ropebuf, n)
                iters = [(hl, qc, m, kt) for hl in range(2) for qc in range(2) for m in range(2)
                         for kt in range(16)]
                pending = []

                def emit_s(idx):
                    hl, qc, m, kt = iters[idx]
                    sb_i = idx % 3
                    pidx = 4 + sb_i
                    pS = PS[pidx]
                    qkeys = [("QT", qc * 4 + q) for q in range(4)]
                    S.op("pe", lambda: nc.tensor.matmul(
                        pS[:, :], lhsT=KT[m * 64:(m + 1) * 64, hl, kt * 128:(kt + 1) * 128],
                        rhs=QT[m * 64:(m + 1) * 64, hl, qc * 512:(qc + 1) * 512],
                        start=True, stop=True), r=[("KT", kt // 8)] + qkeys, w=[("ps", pidx)])
                    S.op("act", lambda: nc.scalar.activation(
                        out=PT[sb_i], in_=pS[:, :], func=AF.Exp),
                        r=[("ps", pidx)], w=[("PT", sb_i)])

                def emit_pv(idx):
                    hl, qc, m, kt = iters[idx]
                    sb_i = idx % 3
                    for q in range(4):
                        S.op("pe", lambda q=q: nc.tensor.matmul(
                            PS[q][:, 0:129], lhsT=PT[sb_i][:, q * 128:(q + 1) * 128],
                            rhs=Va[:, hl, kt, 0:129], start=(kt == 0), stop=(kt == 15)),
                            r=[("PT", sb_i), ("Va", kt // 8), "Va1"], w=[("ps", q)],
                            signal=(q == 3))
                    if kt != 15:
                        return
                    hg = 2 * pa_i + hl
                    if m == 0:
                        for q in range(4):
                            S.op("dve", lambda q=q: V.tensor_copy(
                                out=o0[:, q, 0:129], in_=PS[q][:, 0:129]),
                                r=[("ps", q)], w=[("o0", q)])
                        return
                    for q in range(4):
                        b = q
                        S.op("dve", lambda q=q, b=b: V.reciprocal(
                            out=dst_[:, b, 0:1], in_=o0[:, q, 128:129]),
                            r=[("o0", q)], w=[("d0", b)])
                        S.op("dve", lambda q=q, b=b: V.reciprocal(
                            out=dst_[:, b, 1:2], in_=PS[q][:, 128:129]),
                            r=[("ps", q)], w=[("d1", b)])
                        S.op("dve", lambda b=b: V.tensor_tensor(
                            out=dst_[:, b, 2:3], in0=dst_[:, b, 1:2], in1=small[:, 1:2],
                            op=ALU.mult), r=[("d1", b), "small"], w=[("d2", b)])
                        S.op("dve", lambda q=q, b=b: V.tensor_scalar(
                            out=osb[b], in0=o0[:, q, 0:128], scalar1=dst_[:, b, 0:1],
                            scalar2=None, op0=ALU.mult),
                            r=[("o0", q), ("d0", b)], w=[("osb", b)])
                        S.op("dve", lambda q=q, b=b: V.scalar_tensor_tensor(
                            out=osb[b], in0=PS[q][:, 0:128], scalar=dst_[:, b, 2:3],
                            in1=osb[b], op0=ALU.mult, op1=ALU.add),
                            r=[("ps", q), ("d2", b), ("osb", b)], w=[("osb", b)])
                    for q in range(4):
                        b = q
                        tq = qc * 4 + q
                        S.op("act", lambda b=b: nc.scalar.activation(
                            out=junk2, in_=osb[b], func=AF.Square,
                            accum_out=dst_[:, b, 3:4]),
                            r=[("osb", b)], w=["junk2", ("d3", b)])
                        S.op("act", lambda b=b: nc.scalar.activation(
                            out=dst_[:, b, 4:5], in_=dst_[:, b, 3:4], func=AF.Sqrt,
                            scale=1.0 / 128.0, bias=small[:, 8:9]),
                            r=[("d3", b), "eps"], w=[("d4", b)])
                        S.op("dve", lambda b=b: V.reciprocal(
                            out=dst_[:, b, 5:6], in_=dst_[:, b, 4:5]),
                            r=[("d4", b)], w=[("d5", b)])
                        S.op("dve", lambda b=b: V.scalar_tensor_tensor(
                            out=osb[b], in0=osb[b], scalar=dst_[:, b, 5:6],
                            in1=gsub, op0=ALU.mult, op1=ALU.mult),
                            r=[("osb", b), ("d5", b), "gsub"], w=[("osb", b)])

                        def part2(b=b, tq=tq, hg=hg):
                            S.op("pe", lambda: nc.tensor.transpose(
                                PS[7][:, b * 128:(b + 1) * 128], osb[b], ident),
                                r=[("osb", b), "ident"], w=[("ps7", b)])
                            S.op("act", lambda: nc.scalar.copy(
                                out=mixT[:, hg, tq * 128:(tq + 1) * 128],
                                in_=PS[7][:, b * 128:(b + 1) * 128]),
                                r=[("ps7", b)], w=[("mixT", tq)])
                        pending.append((idx + 4 + q, part2))

                emit_s(0)
                for idx in range(len(iters)):
                    if idx + 1 < len(iters):
                        emit_s(idx + 1)
                    emit_pv(idx)
                    while pending and pending[0][0] <= idx:
                        pending.pop(0)[1]()
                while pending:
                    pending.pop(0)[1]()
                S.barrier()

            RT.reset()
            tmpb = [RT.a([WCOLS], F32) for _ in range(2)]
            ws = WStream([(P["w_out"], 0, WCOLS * n) for n in range(8)])
            u = 0
            for nb_ in range(8):
                wbuf, wkey = ws.get(nb_)
                cs = slice(nb_ * WCOLS, (nb_ + 1) * WCOLS)
                for t in range(NT):
                    pi = u % 2
                    b = u % 2
                    u += 1
                    pb = PS[pi]
                    for k in range(16):
                        S.op("pe", lambda k=k, t=t, pb=pb, wbuf=wbuf: nc.tensor.matmul(
                            pb[:, 0:WCOLS], lhsT=mixT[:, k, t * 128:(t + 1) * 128], rhs=wbuf[:, k, :],
                            start=(k == 0), stop=(k == 15)),
                            r=[wkey, ("mixT", t)], w=[("ps", pi)], signal=(k == 15))
                    S.op("dve", lambda pb=pb, b=b, cs=cs: V.tensor_tensor(
                        out=tmpb[b], in0=pb[:, 0:WCOLS], in1=g1rep[:, cs], op=ALU.mult),
                        r=[("ps", pi), ("grep", 2)], w=[("tmpb", b)])
                    S.op("pool", lambda t=t, b=b, cs=cs: G.tensor_tensor(
                        out=xs[:, t, cs], in0=xs[:, t, cs], in1=tmpb[b], op=ALU.add),
                        r=[("tmpb", b), ("x", t)], w=[("x", t)])
            S.barrier()

            norm_to_hT(1, 16)
            RZ.reset()
            rl = [RZ.a([512], F32) for _ in range(2)]
            RT.reset()
            tmpc = [RT.a([WCOLS], F32) for _ in range(2)]
            specs = []
            for F in range(4):
                specs += [(P["w_up"], 0, F * 2048 + WCOLS * j) for j in range(8)]
                specs += [(P["w_down"], F * 16, WCOLS * j) for j in range(8)]
            ws = WStream(specs)
            hall = [("hT", t) for t in range(NT)]
            u = 0
            e = 0
            for F in range(4):
                for j in range(8):
                    wbuf, wkey = ws.get(F * 16 + j)
                    for sub in range(2):
                        fc = j * 2 + sub
                        for tc in range(2):
                            pi = u % 4
                            u += 1
                            pb = PS[pi]
                            for k in range(16):
                                S.op("pe", lambda k=k, pb=pb, sub=sub, tc=tc, wbuf=wbuf: nc.tensor.matmul(
                                    pb[:, :], lhsT=wbuf[:, k, sub * 128:(sub + 1) * 128],
                                    rhs=hT[:, k, tc * 512:(tc + 1) * 512],
                                    start=(k == 0), stop=(k == 15)),
                                    r=[wkey] + hall[tc * 4:tc * 4 + 4], w=[("ps", pi)], signal=(k == 15))
                            b = pi % 2
                            S.op("act", lambda pb=pb, b=b: nc.scalar.activation(
                                out=rl[b], in_=pb[:, :], func=AF.Relu),
                                r=[("ps", pi)], w=[("rl", b)])
                            S.op("pool", lambda b=b, fc=fc, tc=tc: G.tensor_tensor(
                                out=uT[:, fc, tc * 512:(tc + 1) * 512], in0=rl[b], in1=rl[b],
                                op=ALU.mult), r=[("rl", b)], w=[("uT", tc)])
                for j in range(8):
                    wbuf, wkey = ws.get(F * 16 + 8 + j)
                    cs = slice(j * WCOLS, (j + 1) * WCOLS)
                    for t in range(NT):
                        pi = 4 + (e % 4)
                        b = e % 2
                        e += 1
                        pb = PS[pi]
                        for k in range(16):
                            S.op("pe", lambda k=k, t=t, pb=pb, wbuf=wbuf: nc.tensor.matmul(
                                pb[:, 0:WCOLS], lhsT=uT[:, k, t * 128:(t + 1) * 128], rhs=wbuf[:, k, :],
                                start=(k == 0), stop=(k == 15)),
                                r=[wkey, ("uT", t // 4)], w=[("ps", pi)], signal=(k == 15))
                        S.op("dve", lambda pb=pb, b=b, cs=cs: V.tensor_tensor(
                            out=tmpc[b], in0=pb[:, 0:WCOLS], in1=g2rep[:, cs], op=ALU.mult),
                            r=[("ps", pi), ("grep", 5)], w=[("tmpc", b)])
                        S.op("pool", lambda t=t, b=b, cs=cs: G.tensor_tensor(
                            out=xs[:, t, cs], in0=xs[:, t, cs], in1=tmpc[b], op=ALU.add),
                            r=[("tmpc", b), ("x", t)], w=[("x", t)])
            S.barrier()

        for t in range(NT):
            S.op("sp", lambda t=t: nc.sync.dma_start(out=xs[:, t, :], in_=x_own[t * 128:(t + 1) * 128, :]),
                 w=[("x", t)], dma=("xl", t % 4))
        setup_constants()
        for li, lam_init in enumerate(layer_consts):
            layer(li, lam_init)

        RZ.reset()
        fg = RZ.a([D], F32)
        ob2 = [RZ.a([D], F32) for _ in range(2)]
        RT.reset()
        junk = RT.a([D], BF16)
        S.op("sp", lambda: nc.sync.dma_start(out=fg, in_=fin_g_in[0].partition_broadcast(128)),
             w=["fg"], dma="setup")
        for t in range(NT):
            so = 32 + t
            S.op("act", lambda t=t, so=so: nc.scalar.activation(
                out=junk, in_=xs[:, t, :], func=AF.Square, accum_out=stat[:, so:so + 1]),
                r=[("x", t)], w=["junkf", ("stat", so)])
            S.op("act", lambda so=so: nc.scalar.activation(
                out=stat[:, so:so + 1], in_=stat[:, so:so + 1], func=AF.Sqrt,
                scale=1.0 / D, bias=small[:, 8:9]), r=[("stat", so), "eps"], w=[("stat", so)])
            S.op("dve", lambda so=so: V.reciprocal(out=stat[:, so:so + 1], in_=stat[:, so:so + 1]),
                 r=[("stat", so)], w=[("stat", so)])
            b = t % 2
            S.op("dve", lambda t=t, so=so, b=b: V.scalar_tensor_tensor(
                out=ob2[b], in0=xs[:, t, :], scalar=stat[:, so:so + 1], in1=fg,
                op0=ALU.mult, op1=ALU.mult), r=[("x", t), ("stat", so), "fg"], w=[("fo", b)])
            S.op("sp", lambda t=t, b=b: nc.sync.dma_start(
                out=y_out[t * 128:(t + 1) * 128, :], in_=ob2[b]),
                r=[("fo", b)], w=[("y", t)], dma=("out", b))
        S.barrier()
    return nc


_PROG_CACHE = {}


def _lam_init(layer):
    return 0.8 - 0.6 * math.exp(-0.3 * layer)


def _col(v):
    return np.ascontiguousarray(v.reshape(16, 128).T)


def kernel(x, c, positions, ada_w, ada_b, norm_mix, w_in, diff_lambda, diff_subln,
           swa_sink, w_out, norm_mlp, w_up, w_down, final_norm):
    x = np.asarray(x, dtype=np.float32)
    c = np.asarray(c, dtype=np.float32)
    positions = np.asarray(positions, dtype=np.int32)
    f = lambda a: np.ascontiguousarray(np.asarray(a, dtype=np.float32))
    ada_w, ada_b, norm_mix, w_in = f(ada_w), f(ada_b), f(norm_mix), f(w_in)
    diff_lambda, diff_subln, swa_sink = f(diff_lambda), f(diff_subln), f(swa_sink)
    w_out, norm_mlp, w_up, w_down, final_norm = f(w_out), f(norm_mlp), f(w_up), f(w_down), f(final_norm)

    if "fused" not in _PROG_CACHE:
        _PROG_CACHE["fused"] = build_program([_lam_init(l) for l in range(DEPTH)])
    nc = _PROG_CACHE["fused"]
    in_maps = []
    for core in range(8):
        b, half = core // 2, core % 2
        own = slice(half * 1024, (half + 1) * 1024)
        edge = np.zeros((128, 2), np.float32)
        edge[:, 0] = NEG if half == 0 else 0.0
        edge[:, 1] = NEG if half == 1 else 0.0
        m = {
            "x_own": np.ascontiguousarray(x[b, own]),
            "pos": np.ascontiguousarray(positions[b, own].reshape(NT, 128).T.astype(np.int32)),
            "edge": edge,
            "c_col": _col(c[b]),
            "final_g": final_norm.reshape(1, D),
        }
        for l in range(DEPTH):
            m.update({
                f"ada_w{l}": ada_w[l], f"ada_b{l}": ada_b[l].reshape(1, -1),
                f"nmix{l}": _col(norm_mix[l]), f"nmlp{l}": _col(norm_mlp[l]),
                f"w_in{l}": w_in[l], f"w_out{l}": w_out[l],
                f"w_up{l}": w_up[l], f"w_down{l}": w_down[l],
                f"lam{l}": diff_lambda[l].reshape(1, 256),
                f"subln{l}": diff_subln[l].reshape(1, 128),
                f"sink{l}": swa_sink[l].reshape(1, 8),
            })
        in_maps.append(m)
    res = run_bass_kernel_spmd(nc, in_maps, core_ids=list(range(8)))
    out = np.empty((NB, S_LEN, D), np.float32)
    for core in range(8):
        b, half = core // 2, core % 2
        out[b, half * 1024:(half + 1) * 1024] = res.results[core]["y"]
    return out
```

```python
import math
from contextlib import ExitStack

import numpy as np
import concourse.bass as bass
import concourse.mybir as mybir
from concourse.bass_utils import run_bass_kernel_spmd

F32 = mybir.dt.float32
BF16 = mybir.dt.bfloat16
I32 = mybir.dt.int32
ALU = mybir.AluOpType
AF = mybir.ActivationFunctionType
AX = mybir.AxisListType

D = 2048
S_LEN = 2048
NB = 4
DEPTH = 2
DFF = 8192
INW = 4608
NT = 8
NEG = -30000.0
EPS = 1e-6
WCOLS = 256
NWBUF = 2
TWO_PI = 2.0 * math.pi
C1 = 6.28125
C2 = TWO_PI - C1
MAGIC = 12582912.0


class Sync:
    ENG = ("pe", "act", "dve", "pool", "sp")

    def __init__(self, nc, es):
        self.nc = nc
        self.es = es
        self.eng = {"pe": nc.tensor, "act": nc.scalar, "dve": nc.vector,
                    "pool": nc.gpsimd, "sp": nc.sync}
        self.sem = {}
        self.cnt = {}
        for k in self.ENG:
            self._mk(k)
        self.known = {e: {} for e in self.ENG}
        self.lastw = {}
        self.reads = {}

    def _mk(self, k):
        self.sem[k] = self.es.enter_context(self.nc.semaphore("s_" + "".join(ch for ch in str(k) if ch.isalnum())))
        self.cnt[k] = 0

    def _wait(self, e, deps):
        best = {}
        for sk, c in deps:
            if c > best.get(sk, 0):
                best[sk] = c
        for sk, c in best.items():
            if sk == e:
                if e == "pe":
                    continue
                if self.cnt[e] - c >= 3:
                    continue
            if not isinstance(sk, str) or sk not in self.ENG:
                c = max(c, self.cnt[sk])
            if self.known[e].get(sk, 0) >= c:
                continue
            self.eng[e].wait_ge(self.sem[sk], c)
            self.known[e][sk] = c

    def op(self, e, fn, r=(), w=(), signal=True, dma=None, dma_inc=16):
        deps = []
        for k in r:
            if k in self.lastw:
                deps.append(self.lastw[k])
        for k in w:
            if k in self.lastw:
                deps.append(self.lastw[k])
            rd = self.reads.get(k)
            if rd:
                deps.extend(rd.items())
        self._wait(e, deps)
        ins = fn()
        if dma is not None:
            sk = ("dma", dma)
            if sk not in self.sem:
                self._mk(sk)
            ins.then_inc(self.sem[sk], dma_inc)
            self.cnt[sk] += dma_inc
            c = self.cnt[sk]
        else:
            sk = e
            if signal:
                ins.then_inc(self.sem[sk], 1)
                self.cnt[sk] += 1
                c = self.cnt[sk]
            else:
                c = self.cnt[sk] + 1
        for k in r:
            d = self.reads.setdefault(k, {})
            if c > d.get(sk, 0):
                d[sk] = c
        for k in w:
            self.lastw[k] = (sk, c)
            self.reads[k] = {}
        return ins

    def barrier(self, exclude=()):
        deps = [(sk, c) for sk, c in self.cnt.items() if c > 0 and sk not in exclude]
        for e in self.ENG:
            self._wait(e, deps)
        self.lastw = {k: v for k, v in self.lastw.items() if v[0] in exclude}
        self.reads = {k: {s: c for s, c in d.items() if s in exclude} for k, d in self.reads.items()}


ARENA_WORDS = 53100
KV_ROWS = 2560
CH_ROWS = 512
NCH = KV_ROWS // CH_ROWS
PASSW = 7168


def _view(ar, off, shape, dt, p0=0, p1=128):
    nel = 1
    for s in shape:
        nel *= s
    words = nel if dt in (F32, I32) else (nel + 1) // 2
    assert off + words <= ARENA_WORDS, (off, words)
    ap = ar[p0:p1, off:off + words]
    if dt != F32:
        ap = ap.bitcast(dt)
    if len(shape) == 2:
        ap = ap.rearrange("p (a b) -> p a b", a=shape[0])
    elif len(shape) == 3:
        ap = ap.rearrange("p (a b c) -> p a b c", a=shape[0], b=shape[1])
    return ap, words


class Bump:
    def __init__(self, ar, base, limit):
        self.ar, self.base, self.off, self.limit = ar, base, base, limit

    def a(self, shape, dt, p0=0, p1=128):
        ap, words = _view(self.ar, self.off, shape, dt, p0, p1)
        self.off += (words + 7) // 8 * 8
        assert self.off <= self.limit, (self.off, self.limit)
        return ap

    def reset(self):
        self.off = self.base


def build_program(layer_consts, final_norm=True):
    nc = bass.Bass("TRN2", target_bir_lowering=False)
    nl = len(layer_consts)

    def din(name, shape, dt=F32):
        return nc.dram_tensor(name, list(shape), dt, kind="ExternalInput").ap()

    x_own = din("x_own", [NT * 128, D])
    pos_in = din("pos", [128, NT], I32)
    edge_in = din("edge", [128, 2])
    c_all_in = din("c_all", [128, 64])
    oh_in = din("onehot", [4, 128])
    fin_g_in = din("final_g", [1, D])
    LP = []
    for li in range(nl):
        LP.append(dict(
            ada_w=din(f"ada_wk{li}", [D, 6 * WCOLS]), ada_b=din(f"ada_bk{li}", [1, 6 * WCOLS]),
            mod_own=nc.dram_tensor(f"mod_own{li}", [4, 6 * WCOLS], F32).ap(),
            mod_all=nc.dram_tensor(f"mod_all{li}", [32, 6 * WCOLS], F32).ap(),
            nmix=din(f"nmix{li}", [128, 16]), nmlp=din(f"nmlp{li}", [128, 16]),
            w_in=din(f"w_in{li}", [D, INW]), w_out=din(f"w_out{li}", [D, D]),
            w_up=din(f"w_up{li}", [D, DFF]), w_down=din(f"w_down{li}", [DFF, D]),
            lam=din(f"lam{li}", [1, 256]), subln=din(f"subln{li}", [1, 128]), sink=din(f"sink{li}", [1, 8]),
            kv_own=[nc.dram_tensor(f"kv_own{li}_{c}", [CH_ROWS, 1024], BF16).ap() for c in range(NCH)],
            kv_all=[nc.dram_tensor(f"kv_all{li}_{c}", [2 * CH_ROWS, 1024], BF16).ap() for c in range(NCH)],
        ))
    y_out = nc.dram_tensor("y", [NT * 128, D], F32, kind="ExternalOutput").ap()

    es = ExitStack()
    with es:
        S = Sync(nc, es)
        es.enter_context(nc.Block())
        arena = es.enter_context(nc.sbuf_tensor("arena", [128, ARENA_WORDS], F32))
        PS = [es.enter_context(nc.psum_tensor(f"ps{i}", [128, 512], F32)) for i in range(8)]
        V = nc.vector
        G = nc.gpsimd

        pb_ = Bump(arena, 0, ARENA_WORDS)
        Wb = [pb_.a([16, WCOLS], BF16) for _ in range(NWBUF)]
        cosA = pb_.a([NT, 32], F32)
        sinA = pb_.a([NT, 32], F32)
        cosB = pb_.a([NT, 64], F32)
        sinB = pb_.a([NT, 64], F32)
        mask3 = pb_.a([3, 384], F32)
        g1rep = pb_.a([D], F32)
        g2rep = pb_.a([D], F32)
        ident = pb_.a([128], F32)
        oh = pb_.a([128], F32, 0, 4)
        cact = pb_.a([16, 4], BF16)
        colv = pb_.a([4, 16], F32)
        small = pb_.a([64], F32)
        gsub = pb_.a([128], F32)
        sinkr = pb_.a([8], F32)
        stat = pb_.a([64], F32)
        xs = pb_.a([NT, D], F32)
        HBASE = pb_.off
        MBASE = HBASE + 8192
        ZBASE = MBASE + 8192
        TBASE = ZBASE + PASSW
        TEND = TBASE + 1536
        assert TEND <= ARENA_WORDS, TEND
        RH = Bump(arena, HBASE, MBASE)
        RM = Bump(arena, MBASE, ZBASE)
        RZ = Bump(arena, ZBASE, TBASE)
        RT = Bump(arena, TBASE, TEND)
        hT = RH.a([16, NT * 128], BF16)
        mixT = RM.a([16, NT * 128], BF16)
        uT = mixT

        wstate = {"i": 0}

        def load_wblock(src_ap, k0, n0):
            i = wstate["i"]
            wstate["i"] += 1
            slot = i % NWBUF
            buf = Wb[slot]
            src = src_ap[k0 * 128:(k0 + 16) * 128, n0:n0 + WCOLS].rearrange("(c p) n -> p c n", p=128)
            S.op("pool", lambda: G.dma_start(out=buf, in_=src), w=[("W", slot)], dma=("w", slot))
            return buf, ("W", slot)

        class WStream:
            def __init__(self, specs):
                self.specs, self.issued, self.n = specs, [], 0

            def get(self, i):
                while self.n <= min(i + NWBUF - 1, len(self.specs) - 1):
                    self.issued.append(load_wblock(*self.specs[self.n]))
                    self.n += 1
                return self.issued[i]

        def setup_constants():
            S.op("dve", lambda: V.memset(small[:, 8:9], EPS), w=["eps"])
            S.op("pool", lambda: G.memset(ident, 0.0), w=["ident"])
            S.op("pool", lambda: G.affine_select(
                out=ident, in_=ident, pattern=[[-1, 128]], compare_op=ALU.not_equal,
                fill=1.0, base=0, channel_multiplier=1), r=["ident"], w=["ident"])
            S.op("pool", lambda: G.memset(mask3, 0.0), w=["mask3"])
            for v in range(3):
                S.op("pool", lambda v=v: G.affine_select(
                    out=mask3[:, v, 0:128], in_=mask3[:, v, 0:128], pattern=[[1, 128]],
                    compare_op=ALU.is_ge, fill=NEG, base=0, channel_multiplier=-1),
                    r=["mask3"], w=["mask3"])
                S.op("pool", lambda v=v: G.affine_select(
                    out=mask3[:, v, 256:384], in_=mask3[:, v, 256:384], pattern=[[-1, 128]],
                    compare_op=ALU.is_ge, fill=NEG, base=0, channel_multiplier=1),
                    r=["mask3"], w=["mask3"])
            RZ.reset()
            edge = RZ.a([2], F32)
            posi = RZ.a([NT], I32)
            posf = RZ.a([NT], F32)
            iot = RZ.a([64], F32)
            inv = RZ.a([64], F32)
            ang = RZ.a([NT, 64], F32)
            kk = RZ.a([NT, 64], F32)
            yy = RZ.a([NT, 64], F32)
            ccol = RZ.a([16, 4], F32)
            S.op("sp", lambda: nc.sync.dma_start(out=edge, in_=edge_in), w=["edge"], dma="setup")
            S.op("sp", lambda: nc.sync.dma_start(out=posi, in_=pos_in), w=["posi"], dma="setup")
            S.op("sp", lambda: nc.sync.dma_start(out=ccol.rearrange("p a b -> p (a b)"), in_=c_all_in),
                 w=["ccol"], dma="setup")
            S.op("sp", lambda: nc.sync.dma_start(out=oh, in_=oh_in), w=["oh"], dma="setup")
            S.op("act", lambda: nc.scalar.activation(out=cact, in_=ccol, func=AF.Silu),
                 r=["ccol"], w=["cact"])
            S.op("dve", lambda: V.tensor_scalar(
                out=mask3[:, 1, 0:128], in0=mask3[:, 1, 0:128], scalar1=edge[:, 0:1],
                scalar2=None, op0=ALU.add), r=["mask3", "edge"], w=["mask3"])
            S.op("dve", lambda: V.tensor_scalar(
                out=mask3[:, 2, 256:384], in0=mask3[:, 2, 256:384], scalar1=edge[:, 1:2],
                scalar2=None, op0=ALU.add), r=["mask3", "edge"], w=["mask3"])
            S.op("dve", lambda: V.tensor_copy(out=posf, in_=posi), r=["posi"], w=["posf"])
            S.op("pool", lambda: G.iota(iot, pattern=[[1, 64]], base=0, channel_multiplier=0,
                                        allow_small_or_imprecise_dtypes=True), w=["iot"])
            for (dim, ct, st) in ((64, cosA, sinA), (128, cosB, sinB)):
                hd = dim // 2
                S.op("act", lambda hd=hd, dim=dim: nc.scalar.activation(
                    out=inv[:, 0:hd], in_=iot[:, 0:hd], func=AF.Exp,
                    scale=-(2.0 / dim) * math.log(10000.0)), r=["iot"], w=["inv"])
                for t in range(NT):
                    S.op("dve", lambda t=t, hd=hd: V.tensor_scalar(
                        out=ang[:, t, 0:hd], in0=inv[:, 0:hd], scalar1=posf[:, t:t + 1],
                        scalar2=None, op0=ALU.mult), r=["inv", "posf"], w=["ang"])
                a3 = ang[:, :, 0:hd]
                k3 = kk[:, :, 0:hd]
                y3 = yy[:, :, 0:hd]
                for which in range(2):
                    dst = st if which == 0 else ct
                    if which == 1:
                        S.op("dve", lambda a3=a3: V.tensor_scalar(
                            out=a3, in0=a3, scalar1=0.5 * math.pi, scalar2=None, op0=ALU.add),
                            r=["ang"], w=["ang"])
                    S.op("dve", lambda a3=a3, k3=k3: V.tensor_scalar(
                        out=k3, in0=a3, scalar1=1.0 / TWO_PI, scalar2=MAGIC,
                        op0=ALU.mult, op1=ALU.add), r=["ang"], w=["kk"])
                    S.op("dve", lambda k3=k3: V.tensor_scalar(
                        out=k3, in0=k3, scalar1=-MAGIC, scalar2=None, op0=ALU.add),
                        r=["kk"], w=["kk"])
                    S.op("dve", lambda a3=a3, k3=k3, y3=y3: V.scalar_tensor_tensor(
                        out=y3, in0=k3, scalar=-C1, in1=a3, op0=ALU.mult, op1=ALU.add),
                        r=["kk", "ang"], w=["yy"])
                    S.op("dve", lambda k3=k3, y3=y3: V.scalar_tensor_tensor(
                        out=y3, in0=k3, scalar=-C2, in1=y3, op0=ALU.mult, op1=ALU.add),
                        r=["kk", "yy"], w=["yy"])
                    S.op("dve", lambda y3=y3: V.tensor_scalar(
                        out=y3, in0=y3, scalar1=3.1415925, scalar2=-3.1415925,
                        op0=ALU.min, op1=ALU.max), r=["yy"], w=["yy"])
                    S.op("act", lambda y3=y3, dst=dst: nc.scalar.activation(
                        out=dst, in_=y3, func=AF.Sin), r=["yy"], w=[("tab", dim)])
            S.barrier()

        def mod_phase():
            RZ.reset()
            abk = RZ.a([6 * WCOLS], F32, 0, 4)
            modown = RZ.a([6 * WCOLS], F32, 0, 4)
            for li in range(nl):
                P = LP[li]
                S.op("sp", lambda P=P: nc.sync.dma_start(out=abk, in_=P["ada_b"][0].partition_broadcast(4)),
                     w=["abk"], dma="setup")
                ws = WStream([(P["ada_w"], 0, v * WCOLS) for v in range(6)])
                for v in range(6):
                    wbuf, wkey = ws.get(v)
                    pb = PS[v % 2]
                    for k in range(16):
                        S.op("pe", lambda k=k, wbuf=wbuf, pb=pb: nc.tensor.matmul(
                            pb[0:4, 0:WCOLS], lhsT=cact[:, k, :], rhs=wbuf[:, k, :],
                            start=(k == 0), stop=(k == 15)),
                            r=[wkey, "cact"], w=[("ps", v % 2)], signal=(k == 15))
                    S.op("dve", lambda v=v, pb=pb: V.tensor_tensor(
                        out=modown[:, v * WCOLS:(v + 1) * WCOLS], in0=pb[0:4, 0:WCOLS],
                        in1=abk[:, v * WCOLS:(v + 1) * WCOLS], op=ALU.add),
                        r=[("ps", v % 2), "abk"], w=["modown"])
                S.op("sp", lambda P=P: nc.sync.dma_start(out=P["mod_own"], in_=modown),
                     r=["modown"], w=[("mod_own", li)], dma="setup")
                S.op("pool", lambda P=P: G.collective_compute(
                    "AllGather", op=ALU.bypass, replica_groups=[list(range(8))],
                    ins=[P["mod_own"].opt()], outs=[P["mod_all"].opt()]),
                    r=[("mod_own", li)], w=[("mod_all", li)], dma=("ccm", li), dma_inc=1)
            S.barrier(exclude=[("dma", ("ccm", li)) for li in range(nl)])

        def layer_params(P, lam_init, li):
            RZ.reset()
            lamt = RZ.a([256], F32)
            lprod = RZ.a([128], F32)
            nm = RZ.a([2, 16], F32)
            rowbuf = RZ.a([D], F32, 0, 4)
            S.op("sp", lambda: nc.sync.dma_start(out=lamt, in_=P["lam"][0].partition_broadcast(128)),
                 w=["lamt"], dma="setup")
            S.op("sp", lambda: nc.sync.dma_start(out=gsub, in_=P["subln"][0].partition_broadcast(128)),
                 w=["gsub"], dma="setup")
            S.op("sp", lambda: nc.sync.dma_start(out=sinkr, in_=P["sink"][0].partition_broadcast(128)),
                 w=["sinkr"], dma="setup")
            S.op("sp", lambda: nc.sync.dma_start(out=nm[:, 0, :], in_=P["nmix"]), w=["nm"], dma="setup")
            S.op("sp", lambda: nc.sync.dma_start(out=nm[:, 1, :], in_=P["nmlp"]), w=["nm"], dma="setup")
            S.op("dve", lambda: V.tensor_tensor(out=lprod[:, 0:64], in0=lamt[:, 0:64],
                                                in1=lamt[:, 64:128], op=ALU.mult),
                 r=["lamt"], w=["lprod"])
            S.op("dve", lambda: V.tensor_tensor(out=lprod[:, 64:128], in0=lamt[:, 128:192],
                                                in1=lamt[:, 192:256], op=ALU.mult),
                 r=["lamt"], w=["lprod"])
            S.op("dve", lambda: V.tensor_reduce(out=small[:, 2:4],
                                                in_=lprod.rearrange("p (a b) -> p a b", a=2),
                                                axis=AX.X, op=ALU.add),
                 r=["lprod"], w=["small"])
            S.op("act", lambda: nc.scalar.activation(out=small[:, 4:6], in_=small[:, 2:4], func=AF.Exp),
                 r=["small"], w=["small"])
            S.op("dve", lambda: V.tensor_tensor(out=small[:, 0:1], in0=small[:, 4:5],
                                                in1=small[:, 5:6], op=ALU.subtract),
                 r=["small"], w=["small"])
            S.op("dve", lambda: V.tensor_scalar(out=small[:, 0:1], in0=small[:, 0:1],
                                                scalar1=float(lam_init), scalar2=None, op0=ALU.add),
                 r=["small"], w=["small"])
            S.op("dve", lambda: V.tensor_scalar(out=small[:, 1:2], in0=small[:, 0:1],
                                                scalar1=-1.0, scalar2=None, op0=ALU.mult),
                 r=["small"], w=["small"])
            S.op("dve", lambda: V.tensor_scalar(out=gsub, in0=gsub,
                                                scalar1=float(1.0 - lam_init), scalar2=None,
                                                op0=ALU.mult), r=["gsub"], w=["gsub"])
            mview = P["mod_all"].rearrange("(r b) (v c) -> b v r c", b=4, v=6)
            for v in range(6):
                S.op("sp", lambda v=v: nc.sync.dma_start(
                    out=rowbuf.rearrange("b (r c) -> b r c", r=8), in_=mview[:, v]),
                    r=[("mod_all", li)], w=["rowbuf"], dma="setup")
                if v in (2, 5):
                    dst = g1rep if v == 2 else g2rep
                    for j in range(4):
                        pb = PS[2 + (j % 2)]
                        S.op("pe", lambda j=j, pb=pb: nc.tensor.matmul(
                            pb[:, :], lhsT=oh[0:4, :], rhs=rowbuf[0:4, j * 512:(j + 1) * 512],
                            start=True, stop=True), r=["rowbuf", "oh"], w=[("ps", 2 + j % 2)])
                        S.op("act", lambda j=j, pb=pb, dst=dst: nc.scalar.copy(
                            out=dst[:, j * 512:(j + 1) * 512], in_=pb[:, :]),
                            r=[("ps", 2 + j % 2)], w=[("grep", v)])
                else:
                    pb = PS[4]
                    for c in range(16):
                        S.op("pe", lambda c=c, pb=pb: nc.tensor.matmul(
                            pb[:, c:c + 1], lhsT=rowbuf[0:4, c * 128:(c + 1) * 128],
                            rhs=oh[0:4, 0:1], start=True, stop=True),
                            r=["rowbuf", "oh"], w=[("ps", 4)], signal=(c == 15))
                    half = 0 if v < 3 else 1
                    if v in (0, 3):
                        S.op("dve", lambda pb=pb, half=half: V.tensor_copy(
                            out=colv[:, 2 * half + 1, :], in_=pb[:, 0:16]),
                            r=[("ps", 4)], w=["colv"])
                    else:
                        S.op("dve", lambda pb=pb, half=half: V.scalar_tensor_tensor(
                            out=colv[:, 2 * half, :], in0=pb[:, 0:16], scalar=1.0,
                            in1=nm[:, half, :], op0=ALU.add, op1=ALU.mult),
                            r=[("ps", 4), "nm"], w=["colv"])
            S.barrier()

        def norm_to_hT(which, stat_off):
            RZ.reset()
            xh = [RZ.a([D], F32) for _ in range(2)]
            RT.reset()
            junk = RT.a([D], BF16)
            for t in range(NT):
                xa, xk = xs[:, t, :], ("x", t)
                so = stat_off + t
                S.op("act", lambda xa=xa, so=so: nc.scalar.activation(
                    out=junk, in_=xa, func=AF.Square, accum_out=stat[:, so:so + 1]),
                    r=[xk], w=["junk", ("stat", so)])
                S.op("act", lambda so=so: nc.scalar.activation(
                    out=stat[:, so:so + 1], in_=stat[:, so:so + 1], func=AF.Sqrt,
                    scale=1.0 / D, bias=small[:, 8:9]), r=[("stat", so), "eps"], w=[("stat", so)])
                S.op("dve", lambda so=so: V.reciprocal(out=stat[:, so:so + 1], in_=stat[:, so:so + 1]),
                     r=[("stat", so)], w=[("stat", so)])
                xb = xh[t % 2]
                S.op("dve", lambda xa=xa, xb=xb, so=so: V.tensor_scalar(
                    out=xb, in0=xa, scalar1=stat[:, so:so + 1], scalar2=None, op0=ALU.mult),
                    r=[xk, ("stat", so)], w=[("xh", t % 2)])
                for g4 in range(4):
                    pi = g4
                    pb = PS[pi]
                    for i in range(4):
                        c = g4 * 4 + i
                        S.op("pe", lambda c=c, i=i, pb=pb, xb=xb: nc.tensor.transpose(
                            pb[:, i * 128:(i + 1) * 128], xb[:, c * 128:(c + 1) * 128], ident),
                            r=[("xh", t % 2), "ident"], w=[("ps", pi)], signal=(i == 3))
                    for i in range(4):
                        c = g4 * 4 + i
                        dst = hT[:, c, t * 128:(t + 1) * 128]
                        src = pb[:, i * 128:(i + 1) * 128]
                        if i % 2 == 0:
                            S.op("act", lambda dst=dst, src=src, c=c: nc.scalar.activation(
                                out=dst, in_=src, func=AF.Identity,
                                scale=colv[:, 2 * which, c:c + 1], bias=colv[:, 2 * which + 1, c:c + 1]),
                                r=[("ps", pi), "colv"], w=[("hT", t)])
                        else:
                            S.op("dve", lambda dst=dst, src=src, c=c: V.tensor_scalar(
                                out=dst, in0=src, scalar1=colv[:, 2 * which, c:c + 1],
                                scalar2=colv[:, 2 * which + 1, c:c + 1], op0=ALU.mult, op1=ALU.add),
                                r=[("ps", pi), "colv"], w=[("hT", t)])
            S.barrier()

        pstate = {"n": 0}

        def proj_block(wbuf, wkey):
            for t in range(NT):
                n = pstate["n"]
                pstate["n"] += 1
                pi = n % 2
                pb = PS[pi]
                for k in range(16):
                    S.op("pe", lambda k=k, t=t, pb=pb: nc.tensor.matmul(
                        pb[:, 0:WCOLS], lhsT=hT[:, k, t * 128:(t + 1) * 128], rhs=wbuf[:, k, :],
                        start=(k == 0), stop=(k == 15)),
                        r=[wkey, ("hT", t)], w=[("ps", pi)], signal=(k == 15))
                flush_rope()
                yield n, t, pb[:, 0:WCOLS], ("ps", pi)
            flush_rope()

        def rope_to_T(ps_ap, pskey, slot, dim, scale, cos_t, sin_t, dsts, ropebuf, n):
            b = n % 2
            xs_, t1, t2 = ropebuf[b]
            ng = WCOLS // dim
            hd = dim // 2
            S.op("act", lambda: nc.scalar.activation(out=xs_, in_=ps_ap, func=AF.Copy, scale=float(scale)),
                 r=[pskey], w=[("ropeX", b)])
            x4 = xs_.rearrange("p (g h d) -> p g h d", g=ng, h=2)
            a4 = t1.rearrange("p (g h d) -> p g h d", g=ng, h=2)
            b4 = t2.rearrange("p (g h d) -> p g h d", g=ng, h=2)
            cb = cos_t[:, slot, :].unsqueeze(1).unsqueeze(1).to_broadcast([128, ng, 2, hd])
            sb1 = sin_t[:, slot, :].unsqueeze(1).to_broadcast([128, ng, hd])
            S.op("dve", lambda: V.tensor_tensor(out=a4, in0=x4, in1=cb, op=ALU.mult),
                 r=[("ropeX", b), ("tab", dim)], w=[("ropeA", b)])
            S.op("pool", lambda: G.tensor_tensor(out=b4[:, :, 0, :], in0=x4[:, :, 1, :], in1=sb1, op=ALU.mult),
                 r=[("ropeX", b), ("tab", dim)], w=[("ropeB", b)])
            S.op("pool", lambda: G.tensor_tensor(out=b4[:, :, 1, :], in0=x4[:, :, 0, :], in1=sb1, op=ALU.mult),
                 r=[("ropeX", b), ("tab", dim)], w=[("ropeB", b)])
            S.op("dve", lambda: V.tensor_tensor(out=a4[:, :, 0, :], in0=a4[:, :, 0, :], in1=b4[:, :, 0, :],
                                                op=ALU.subtract),
                 r=[("ropeA", b), ("ropeB", b)], w=[("ropeA", b)])
            S.op("dve", lambda: V.tensor_tensor(out=a4[:, :, 1, :], in0=a4[:, :, 1, :], in1=b4[:, :, 1, :],
                                                op=ALU.add),
                 r=[("ropeA", b), ("ropeB", b)], w=[("ropeA", b)])
            pi = 2 + b
            pb = PS[pi]

            def tail():
                for j in range(2):
                    S.op("pe", lambda j=j: nc.tensor.transpose(
                        pb[:, j * 128:(j + 1) * 128], t1[:, j * 128:(j + 1) * 128], ident),
                        r=[("ropeA", b), "ident"], w=[("ps", pi)], signal=(j == 1))
                for j in range(2):
                    dst, dk = dsts[j]
                    if j == 0:
                        S.op("act", lambda dst=dst: nc.scalar.copy(out=dst, in_=pb[:, 0:128]),
                             r=[("ps", pi)], w=[dk])
                    else:
                        S.op("dve", lambda dst=dst: V.tensor_copy(out=dst, in_=pb[:, 128:256]),
                             r=[("ps", pi)], w=[dk])
            rope_tail.append(tail)

        rope_tail = []

        def flush_rope(keep=0):
            while len(rope_tail) > keep:
                rope_tail.pop(0)()

        def make_ropebuf():
            RT.reset()
            return [tuple(RT.a([WCOLS], F32) for _ in range(3)) for _ in range(2)]

        def layer(li, lam_init):
            P = LP[li]
            kv_own, kv_all = P["kv_own"], P["kv_all"]
            ccsems = [("dma", ("cc", li, c)) for c in range(NCH)]

            def allgather(c):
                S.op("pool", lambda: G.collective_compute(
                    "AllGather", op=ALU.bypass, replica_groups=[[0, 1], [2, 3], [4, 5], [6, 7]],
                    ins=[kv_own[c].opt()], outs=[kv_all[c].opt()]),
                    r=[("kv_own", c)], w=[("kv_all", c)], dma=("cc", li, c), dma_inc=1)
            layer_params(P, lam_init, li)
            norm_to_hT(0, 0)
            ropebuf = make_ropebuf()

            RZ.reset()
            KbT = RZ.a([2, NT * 128], BF16)
            Vb = RZ.a([NT, 256], BF16)
            zmark = RZ.off
            KTs = [RZ.a([2, NT * 128], BF16) for _ in range(2)]
            Vs = [RZ.a([NT, 256], BF16) for _ in range(2)]
            ws = WStream([(P["w_in"], 0, 1024 + 256 * j) for j in range(4)] +
                         [(P["w_in"], 0, 2048 + 256 * j) for j in range(4)] +
                         [(P["w_in"], 0, 4096), (P["w_in"], 0, 4352)] +
                         [(P["w_in"], 0, 3072 + 256 * j) for j in range(4)])
            for j in range(4):
                wbuf, wkey = ws.get(j)
                st = KTs[j % 2]
                for n, t, pa, pk in proj_block(wbuf, wkey):
                    dsts = [(st[:, q, t * 128:(t + 1) * 128], ("KTs", j % 2)) for q in range(2)]
                    rope_to_T(pa, pk, t, 64, 1.0, cosA, sinA, dsts, ropebuf, n)
                S.op("sp", lambda j=j, st=st: nc.sync.dma_start(
                    out=kv_own[j // 2][(j % 2) * 256:(j % 2) * 256 + 256, :].rearrange("(q p) c -> p q c", p=128),
                    in_=st), r=[("KTs", j % 2)], w=[("kv_own", j // 2)], dma=("kvs", j % 2))
                if j % 2 == 1:
                    allgather(j // 2)
            for j in range(4):
                wbuf, wkey = ws.get(4 + j)
                st = Vs[j % 2]
                for n, t, pa, pk in proj_block(wbuf, wkey):
                    S.op("act", lambda pa=pa, t=t, st=st: nc.scalar.copy(out=st[:, t, :], in_=pa),
                         r=[pk], w=[("Vs", j % 2)])
                for hh in range(2):
                    S.op("sp", lambda j=j, st=st, hh=hh: nc.sync.dma_start(
                        out=kv_own[2 + hh][:, j * 256:(j + 1) * 256].rearrange("(t p) c -> p t c", p=128),
                        in_=st[:, hh * 4:(hh + 1) * 4, :]), r=[("Vs", j % 2)], w=[("kv_own", 2 + hh)],
                        dma=("kvs", 2 + j % 2))
            allgather(2)
            allgather(3)
            wbuf, wkey = ws.get(8)
            for n, t, pa, pk in proj_block(wbuf, wkey):
                dsts = [(KbT[:, g, t * 128:(t + 1) * 128], ("KbT", t)) for g in range(2)]
                rope_to_T(pa, pk, t, 128, 1.0, cosB, sinB, dsts, ropebuf, n)
            S.op("sp", lambda: nc.sync.dma_start(
                out=kv_own[4][0:256, :].rearrange("(g p) c -> p g c", p=128), in_=KbT),
                r=[("KbT", t) for t in range(NT)], w=[("kv_own", 4)], dma=("kvs", 0))
            wbuf, wkey = ws.get(9)
            for n, t, pa, pk in proj_block(wbuf, wkey):
                S.op("act", lambda pa=pa, t=t: nc.scalar.copy(out=Vb[:, t, :], in_=pa), r=[pk], w=[("Vb", t)])
            S.op("sp", lambda: nc.sync.dma_start(
                out=kv_own[4][256:512, :].rearrange("a (f c) -> (a f) c", f=4).rearrange(
                    "(t p) c -> p t c", p=128), in_=Vb),
                r=[("Vb", t) for t in range(NT)], w=[("kv_own", 4)], dma=("kvs", 1))
            allgather(4)

            RZ.off = zmark
            QbT = RZ.a([4, NT * 128], BF16)
            Sm = [RZ.a([384], F32) for _ in range(2)]
            Ee = [RZ.a([384], F32) for _ in range(2)]
            ET = [RZ.a([3, 128], BF16) for _ in range(2)]
            ob = [RZ.a([128], F32) for _ in range(2)]
            sst = RZ.a([4, 8], F32)
            KbTd = RZ.a([2, 256], BF16)
            Vbd = RZ.a([2, 256], BF16)
            S.barrier(exclude=ccsems)

            def vb_sec(r):
                return kv_all[4][r * CH_ROWS + 256:r * CH_ROWS + 512, :].rearrange("a (f c) -> (a f) c", f=4)

            S.op("sp", lambda: nc.sync.dma_start(
                out=KbTd[:, :, 0:128],
                in_=kv_all[4][0:256, 896:1024].rearrange("(g p) c -> p g c", p=128)),
                r=[("kv_all", 4)], w=["KbTd"], dma="bd")
            S.op("sp", lambda: nc.sync.dma_start(
                out=KbTd[:, :, 128:256],
                in_=kv_all[4][CH_ROWS:CH_ROWS + 256, 0:128].rearrange("(g p) c -> p g c", p=128)),
                r=[("kv_all", 4)], w=["KbTd"], dma="bd")
            S.op("sp", lambda: nc.sync.dma_start(out=Vbd[:, 0, :], in_=vb_sec(0)[896:1024, :]),
                 r=[("kv_all", 4)], w=["Vbd"], dma="bd")
            S.op("sp", lambda: nc.sync.dma_start(out=Vbd[:, 1, :], in_=vb_sec(1)[0:128, :]),
                 r=[("kv_all", 4)], w=["Vbd"], dma="bd")

            def kb_src(sl, g):
                if sl == "prev":
                    return KbTd[:, g, 0:128], "KbTd"
                if sl == "next":
                    return KbTd[:, g, 128:256], "KbTd"
                return KbT[:, g, sl * 128:(sl + 1) * 128], ("KbT", sl)

            def vb_src(sl, g):
                if sl == "prev":
                    return Vbd[:, 0, g * 128:(g + 1) * 128], "Vbd"
                if sl == "next":
                    return Vbd[:, 1, g * 128:(g + 1) * 128], "Vbd"
                return Vb[:, sl, g * 128:(g + 1) * 128], ("Vb", sl)

            for g in range(2):
                for j in range(2):
                    wbuf, wkey = ws.get(10 + 2 * g + j)
                    for n, t, pa, pk in proj_block(wbuf, wkey):
                        dsts = [(QbT[:, 2 * j + q, t * 128:(t + 1) * 128], ("QbT", t)) for q in range(2)]
                        rope_to_T(pa, pk, t, 128, 128.0 ** -0.5, cosB, sinB, dsts, ropebuf, n)
                units = [(i, hl) for i in range(NT) for hl in range(4)]

                def stage_a(u, g=g):
                    i, hl = units[u]
                    b = u % 2
                    slots = ((i - 1) if i > 0 else "prev", i, (i + 1) if i < NT - 1 else "next")
                    pS = PS[4 + b]
                    for bi_, sl in enumerate(slots):
                        ksrc, kkey = kb_src(sl, g)
                        S.op("pe", lambda bi_=bi_, ksrc=ksrc, pS=pS, hl=hl, i=i: nc.tensor.matmul(
                            pS[:, bi_ * 128:(bi_ + 1) * 128], lhsT=QbT[:, hl, i * 128:(i + 1) * 128],
                            rhs=ksrc, start=True, stop=True),
                            r=[("QbT", i), kkey], w=[("ps", 4 + b)], signal=(bi_ == 2))

                def stage_b1(u, g=g):
                    i, hl = units[u]
                    b = u % 2
                    sb4 = u % 4
                    h = 4 * g + hl
                    mv = 1 if i == 0 else (2 if i == NT - 1 else 0)
                    pS = PS[4 + b]
                    S.op("dve", lambda: V.tensor_tensor(
                        out=Sm[b], in0=pS[:, 0:384], in1=mask3[:, mv, :], op=ALU.add),
                        r=[("ps", 4 + b), "mask3"], w=[("Sm", b)])
                    S.op("dve", lambda: V.tensor_reduce(
                        out=sst[:, sb4, 0:1], in_=Sm[b], axis=AX.X, op=ALU.max),
                        r=[("Sm", b)], w=[("sst", sb4)])
                    S.op("dve", lambda: V.tensor_scalar(
                        out=sst[:, sb4, 1:2], in0=sst[:, sb4, 0:1], scalar1=sinkr[:, h:h + 1],
                        scalar2=-1.0, op0=ALU.max, op1=ALU.mult),
                        r=[("sst", sb4), "sinkr"], w=[("sst1", sb4)])

                def stage_b2(u, g=g):
                    i, hl = units[u]
                    b = u % 2
                    sb4 = u % 4
                    h = 4 * g + hl
                    S.op("act", lambda: nc.scalar.activation(
                        out=Ee[b], in_=Sm[b], func=AF.Exp, bias=sst[:, sb4, 1:2],
                        accum_out=sst[:, sb4, 2:3]),
                        r=[("Sm", b), ("sst1", sb4)], w=[("Ee", b), ("sst2", sb4)])
                    S.op("act", lambda: nc.scalar.activation(
                        out=sst[:, sb4, 3:4], in_=sinkr[:, h:h + 1], func=AF.Exp, bias=sst[:, sb4, 1:2]),
                        r=[("sst1", sb4), "sinkr"], w=[("sst3", sb4)])
                    S.op("dve", lambda: V.tensor_tensor(
                        out=sst[:, sb4, 4:5], in0=sst[:, sb4, 2:3], in1=sst[:, sb4, 3:4], op=ALU.add),
                        r=[("sst2", sb4), ("sst3", sb4)], w=[("sst4", sb4)])
                    S.op("dve", lambda: V.reciprocal(out=sst[:, sb4, 5:6], in_=sst[:, sb4, 4:5]),
                         r=[("sst4", sb4)], w=[("sst5", sb4)])

                def stage_c(u, g=g):
                    i, hl = units[u]
                    b = u % 2
                    slots = ((i - 1) if i > 0 else "prev", i, (i + 1) if i < NT - 1 else "next")
                    pT = PS[6 + b]
                    for bi_ in range(3):
                        S.op("pe", lambda bi_=bi_: nc.tensor.transpose(
                            pT[:, bi_ * 128:(bi_ + 1) * 128], Ee[b][:, bi_ * 128:(bi_ + 1) * 128],
                            ident), r=[("Ee", b), "ident"], w=[("ps", 6 + b)], signal=(bi_ == 2))
                    S.op("act", lambda: nc.scalar.copy(
                        out=ET[b].rearrange("p a b -> p (a b)"), in_=pT[:, 0:384]),
                        r=[("ps", 6 + b)], w=[("ET", b)])

                def stage_d(u, g=g):
                    i, hl = units[u]
                    b = u % 2
                    h = 4 * g + hl
                    slots = ((i - 1) if i > 0 else "prev", i, (i + 1) if i < NT - 1 else "next")
                    pO = PS[2 + b]
                    for bi_, sl in enumerate(slots):
                        vsrc, vkey = vb_src(sl, g)
                        S.op("pe", lambda bi_=bi_, vsrc=vsrc: nc.tensor.matmul(
                            pO[:, 0:128], lhsT=ET[b][:, bi_, :], rhs=vsrc,
                            start=(bi_ == 0), stop=(bi_ == 2)),
                            r=[("ET", b), vkey], w=[("ps", 2 + b)], signal=(bi_ == 2))
                    S.op("act", lambda: nc.scalar.activation(
                        out=ob[b], in_=pO[:, 0:128], func=AF.Identity, scale=sst[:, u % 4, 5:6]),
                        r=[("ps", 2 + b), ("sst5", u % 4)], w=[("ob", b)])

                def stage_e(u, g=g):
                    i, hl = units[u]
                    b = u % 2
                    h = 4 * g + hl
                    pM = PS[b]
                    S.op("pe", lambda: nc.tensor.transpose(pM[:, 0:128], ob[b], ident),
                         r=[("ob", b), "ident"], w=[("ps", b)])
                    S.op("dve", lambda: V.tensor_copy(
                        out=mixT[:, 8 + h, i * 128:(i + 1) * 128], in_=pM[:, 0:128]),
                        r=[("ps", b)], w=[("mixT", i)])

                nu = len(units)
                for step in range(nu + 5):
                    if step < nu:
                        stage_a(step)
                    if 1 <= step <= nu:
                        stage_b1(step - 1)
                    if 2 <= step <= nu + 1:
                        stage_b2(step - 2)
                    if 3 <= step <= nu + 2:
                        stage_c(step - 3)
                    if 4 <= step <= nu + 3:
                        stage_d(step - 4)
                    if 5 <= step <= nu + 4:
                        stage_e(step - 5)
            S.barrier()

            for pa_i in range(4):
                RZ.reset()
                KT = RZ.a([2, 16 * 128], BF16)
                Va = RZ.a([2, 16, 132], BF16)
                QT = RZ.a([2, NT * 128], BF16)
                PT = [RZ.a([512], BF16) for _ in range(3)]
                o0 = RZ.a([4, 132], F32)
                osb = [RZ.a([128], F32) for _ in range(4)]
                dst_ = RZ.a([4, 8], F32)
                junk2 = RZ.a([128], F32)
                S.op("pool", lambda: G.memset(Va[:, :, :, 128:129], 1.0), w=["Va1"])
                for r_ in range(2):
                    for hl in range(2):
                        hg = 2 * pa_i + hl
                        S.op("sp", lambda r_=r_, hl=hl, hg=hg: nc.sync.dma_start(
                            out=KT[:, hl, r_ * 1024:(r_ + 1) * 1024],
                            in_=kv_all[hg // 4][r_ * CH_ROWS + (hg % 4) * 128:r_ * CH_ROWS + (hg % 4 + 1) * 128, :]),
                            r=[("kv_all", hg // 4)], w=[("KT", r_)], dma=("kvl", r_))
                    for hl in range(2):
                        hg = 2 * pa_i + hl
                        for hh in range(2):
                            S.op("sp", lambda r_=r_, hl=hl, hg=hg, hh=hh: nc.sync.dma_start(
                                out=Va[:, hl, r_ * 8 + hh * 4:r_ * 8 + hh * 4 + 4, 0:128],
                                in_=kv_all[2 + hh][r_ * CH_ROWS:(r_ + 1) * CH_ROWS,
                                                   hg * 128:(hg + 1) * 128].rearrange("(t p) d -> p t d", p=128)),
                                r=[("kv_all", 2 + hh)], w=[("Va", r_)], dma=("kvl", 2 + r_))
                ws = WStream([(P["w_in"], 0, 256 * pa_i)])
                wbuf, wkey = ws.get(0)
                for n, t, pa, pk in proj_block(wbuf, wkey):
                    dsts = [(QT[:, q, t * 128:(t + 1) * 128], ("QT", t)) for q in range(2)]
                    rope_to_T(pa, pk, t, 64, 0.125, cosA, sinA, dsts, ropebuf, n)
                iters = [(hl, qc, m, kt) for hl in range(2) for qc in range(2) for m in range(2)
                         for kt in range(16)]
                pending = []

                def emit_s(idx):
                    hl, qc, m, kt = iters[idx]
                    sb_i = idx % 3
                    pidx = 4 + sb_i
                    pS = PS[pidx]
                    qkeys = [("QT", qc * 4 + q) for q in range(4)]
                    S.op("pe", lambda: nc.tensor.matmul(
                        pS[:, :], lhsT=KT[m * 64:(m + 1) * 64, hl, kt * 128:(kt + 1) * 128],
                        rhs=QT[m * 64:(m + 1) * 64, hl, qc * 512:(qc + 1) * 512],
                        start=True, stop=True), r=[("KT", kt // 8)] + qkeys, w=[("ps", pidx)])
                    S.op("act", lambda: nc.scalar.activation(
                        out=PT[sb_i], in_=pS[:, :], func=AF.Exp),
                        r=[("ps", pidx)], w=[("PT", sb_i)])

                def emit_pv(idx):
                    hl, qc, m, kt = iters[idx]
                    sb_i = idx % 3
                    for q in range(4):
                        S.op("pe", lambda q=q: nc.tensor.matmul(
                            PS[q][:, 0:129], lhsT=PT[sb_i][:, q * 128:(q + 1) * 128],
                            rhs=Va[:, hl, kt, 0:129], start=(kt == 0), stop=(kt == 15)),
                            r=[("PT", sb_i), ("Va", kt // 8), "Va1"], w=[("ps", q)],
                            signal=(q == 3))
                    if kt != 15:
                        return
                    hg = 2 * pa_i + hl
                    if m == 0:
                        for q in range(4):
                            S.op("dve", lambda q=q: V.tensor_copy(
                                out=o0[:, q, 0:129], in_=PS[q][:, 0:129]),
                                r=[("ps", q)], w=[("o0", q)])
                        return
                    for q in range(4):
                        b = q
                        S.op("dve", lambda q=q, b=b: V.reciprocal(
                            out=dst_[:, b, 0:1], in_=o0[:, q, 128:129]),
                            r=[("o0", q)], w=[("d0", b)])
                        S.op("dve", lambda q=q, b=b: V.reciprocal(
                            out=dst_[:, b, 1:2], in_=PS[q][:, 128:129]),
                            r=[("ps", q)], w=[("d1", b)])
                        S.op("dve", lambda b=b: V.tensor_tensor(
                            out=dst_[:, b, 2:3], in0=dst_[:, b, 1:2], in1=small[:, 1:2],
                            op=ALU.mult), r=[("d1", b), "small"], w=[("d2", b)])
                        S.op("dve", lambda q=q, b=b: V.tensor_scalar(
                            out=osb[b], in0=o0[:, q, 0:128], scalar1=dst_[:, b, 0:1],
                            scalar2=None, op0=ALU.mult),
                            r=[("o0", q), ("d0", b)], w=[("osb", b)])
                        S.op("dve", lambda q=q, b=b: V.scalar_tensor_tensor(
                            out=osb[b], in0=PS[q][:, 0:128], scalar=dst_[:, b, 2:3],
                            in1=osb[b], op0=ALU.mult, op1=ALU.add),
                            r=[("ps", q), ("d2", b), ("osb", b)], w=[("osb", b)])
                    for q in range(4):
                        b = q
                        tq = qc * 4 + q
                        S.op("act", lambda b=b: nc.scalar.activation(
                            out=junk2, in_=osb[b], func=AF.Square,
                            accum_out=dst_[:, b, 3:4]),
                            r=[("osb", b)], w=["junk2", ("d3", b)])
                        S.op("act", lambda b=b: nc.scalar.activation(
                            out=dst_[:, b, 4:5], in_=dst_[:, b, 3:4], func=AF.Sqrt,
                            scale=1.0 / 128.0, bias=small[:, 8:9]),
                            r=[("d3", b), "eps"], w=[("d4", b)])
                        S.op("dve", lambda b=b: V.reciprocal(
                            out=dst_[:, b, 5:6], in_=dst_[:, b, 4:5]),
                            r=[("d4", b)], w=[("d5", b)])
                        S.op("dve", lambda b=b: V.scalar_tensor_tensor(
                            out=osb[b], in0=osb[b], scalar=dst_[:, b, 5:6],
                            in1=gsub, op0=ALU.mult, op1=ALU.mult),
                            r=[("osb", b), ("d5", b), "gsub"], w=[("osb", b)])

                        def part2(b=b, tq=tq, hg=hg):
                            S.op("pe", lambda: nc.tensor.transpose(
                                PS[7][:, b * 128:(b + 1) * 128], osb[b], ident),
                                r=[("osb", b), "ident"], w=[("ps", 7)])
                            S.op("act", lambda: nc.scalar.copy(
                                out=mixT[:, hg, tq * 128:(tq + 1) * 128],
                                in_=PS[7][:, b * 128:(b + 1) * 128]),
                                r=[("ps", 7)], w=[("mixT", tq)])
                        pending.append((idx + 4 + q, part2))

                emit_s(0)
                for idx in range(len(iters)):
                    if idx + 1 < len(iters):
                        emit_s(idx + 1)
                    emit_pv(idx)
                    while pending and pending[0][0] <= idx:
                        pending.pop(0)[1]()
                while pending:
                    pending.pop(0)[1]()
                S.barrier()

            RT.reset()
            tmpb = [RT.a([WCOLS], F32) for _ in range(2)]
            ws = WStream([(P["w_out"], 0, WCOLS * n) for n in range(8)])
            u = 0
            for nb_ in range(8):
                wbuf, wkey = ws.get(nb_)
                cs = slice(nb_ * WCOLS, (nb_ + 1) * WCOLS)
                for t in range(NT):
                    pi = u % 2
                    b = u % 2
                    u += 1
                    pb = PS[pi]
                    for k in range(16):
                        S.op("pe", lambda k=k, t=t, pb=pb, wbuf=wbuf: nc.tensor.matmul(
                            pb[:, 0:WCOLS], lhsT=mixT[:, k, t * 128:(t + 1) * 128], rhs=wbuf[:, k, :],
                            start=(k == 0), stop=(k == 15)),
                            r=[wkey, ("mixT", t)], w=[("ps", pi)], signal=(k == 15))
                    S.op("dve", lambda pb=pb, b=b, cs=cs: V.tensor_tensor(
                        out=tmpb[b], in0=pb[:, 0:WCOLS], in1=g1rep[:, cs], op=ALU.mult),
                        r=[("ps", pi), ("grep", 2)], w=[("tmpb", b)])
                    S.op("pool", lambda t=t, b=b, cs=cs: G.tensor_tensor(
                        out=xs[:, t, cs], in0=xs[:, t, cs], in1=tmpb[b], op=ALU.add),
                        r=[("tmpb", b), ("x", t)], w=[("x", t)])
            S.barrier()

            norm_to_hT(1, 16)
            RZ.reset()
            rl = [RZ.a([512], F32) for _ in range(2)]
            RT.reset()
            tmpc = [RT.a([WCOLS], F32) for _ in range(2)]
            specs = []
            for F in range(4):
                specs += [(P["w_up"], 0, F * 2048 + WCOLS * j) for j in range(8)]
                specs += [(P["w_down"], F * 16, WCOLS * j) for j in range(8)]
            ws = WStream(specs)
            hall = [("hT", t) for t in range(NT)]
            u = 0
            e = 0
            for F in range(4):
                for j in range(8):
                    wbuf, wkey = ws.get(F * 16 + j)
                    for sub in range(2):
                        fc = j * 2 + sub
                        for tc in range(2):
                            pi = u % 4
                            u += 1
                            pb = PS[pi]
                            for k in range(16):
                                S.op("pe", lambda k=k, pb=pb, sub=sub, tc=tc, wbuf=wbuf: nc.tensor.matmul(
                                    pb[:, :], lhsT=wbuf[:, k, sub * 128:(sub + 1) * 128],
                                    rhs=hT[:, k, tc * 512:(tc + 1) * 512],
                                    start=(k == 0), stop=(k == 15)),
                                    r=[wkey] + hall[tc * 4:tc * 4 + 4], w=[("ps", pi)], signal=(k == 15))
                            b = pi % 2
                            S.op("act", lambda pb=pb, b=b: nc.scalar.activation(
                                out=rl[b], in_=pb[:, :], func=AF.Relu),
                                r=[("ps", pi)], w=[("rl", b)])
                            S.op("pool", lambda b=b, fc=fc, tc=tc: G.tensor_tensor(
                                out=uT[:, fc, tc * 512:(tc + 1) * 512], in0=rl[b], in1=rl[b],
                                op=ALU.mult), r=[("rl", b)], w=[("uT", tc)])
                for j in range(8):
                    wbuf, wkey = ws.get(F * 16 + 8 + j)
                    cs = slice(j * WCOLS, (j + 1) * WCOLS)
                    for t in range(NT):
                        pi = 4 + (e % 4)
                        b = e % 2
                        e += 1
                        pb = PS[pi]
                        for k in range(16):
                            S.op("pe", lambda k=k, t=t, pb=pb, wbuf=wbuf: nc.tensor.matmul(
                                pb[:, 0:WCOLS], lhsT=uT[:, k, t * 128:(t + 1) * 128], rhs=wbuf[:, k, :],
                                start=(k == 0), stop=(k == 15)),
                                r=[wkey, ("uT", t // 4)], w=[("ps", pi)], signal=(k == 15))
                        S.op("dve", lambda pb=pb, b=b, cs=cs: V.tensor_tensor(
                            out=tmpc[b], in0=pb[:, 0:WCOLS], in1=g2rep[:, cs], op=ALU.mult),
                            r=[("ps", pi), ("grep", 5)], w=[("tmpc", b)])
                        S.op("pool", lambda t=t, b=b, cs=cs: G.tensor_tensor(
                            out=xs[:, t, cs], in0=xs[:, t, cs], in1=tmpc[b], op=ALU.add),
                            r=[("tmpc", b), ("x", t)], w=[("x", t)])
            S.barrier()

        for t in range(NT):
            S.op("sp", lambda t=t: nc.sync.dma_start(out=xs[:, t, :], in_=x_own[t * 128:(t + 1) * 128, :]),
                 w=[("x", t)], dma=("xl", t % 4))
        setup_constants()
        mod_phase()
        for li, lam_init in enumerate(layer_consts):
            layer(li, lam_init)

        RZ.reset()
        fg = RZ.a([D], F32)
        ob2 = [RZ.a([D], F32) for _ in range(2)]
        RT.reset()
        junk = RT.a([D], BF16)
        S.op("sp", lambda: nc.sync.dma_start(out=fg, in_=fin_g_in[0].partition_broadcast(128)),
             w=["fg"], dma="setup")
        for t in range(NT):
            so = 32 + t
            S.op("act", lambda t=t, so=so: nc.scalar.activation(
                out=junk, in_=xs[:, t, :], func=AF.Square, accum_out=stat[:, so:so + 1]),
                r=[("x", t)], w=["junkf", ("stat", so)])
            S.op("act", lambda so=so: nc.scalar.activation(
                out=stat[:, so:so + 1], in_=stat[:, so:so + 1], func=AF.Sqrt,
                scale=1.0 / D, bias=small[:, 8:9]), r=[("stat", so), "eps"], w=[("stat", so)])
            S.op("dve", lambda so=so: V.reciprocal(out=stat[:, so:so + 1], in_=stat[:, so:so + 1]),
                 r=[("stat", so)], w=[("stat", so)])
            b = t % 2
            S.op("dve", lambda t=t, so=so, b=b: V.scalar_tensor_tensor(
                out=ob2[b], in0=xs[:, t, :], scalar=stat[:, so:so + 1], in1=fg,
                op0=ALU.mult, op1=ALU.mult), r=[("x", t), ("stat", so), "fg"], w=[("fo", b)])
            S.op("sp", lambda t=t, b=b: nc.sync.dma_start(
                out=y_out[t * 128:(t + 1) * 128, :], in_=ob2[b]),
                r=[("fo", b)], w=[("y", t)], dma=("out", b))
        S.barrier()
    return nc


_PROG_CACHE = {}


def _lam_init(layer):
    return 0.8 - 0.6 * math.exp(-0.3 * layer)


def _col(v):
    return np.ascontiguousarray(v.reshape(16, 128).T)


def kernel(x, c, positions, ada_w, ada_b, norm_mix, w_in, diff_lambda, diff_subln,
           swa_sink, w_out, norm_mlp, w_up, w_down, final_norm):
    x = np.asarray(x, dtype=np.float32)
    c = np.asarray(c, dtype=np.float32)
    positions = np.asarray(positions, dtype=np.int32)
    f = lambda a: np.ascontiguousarray(np.asarray(a, dtype=np.float32))
    ada_w, ada_b, norm_mix, w_in = f(ada_w), f(ada_b), f(norm_mix), f(w_in)
    diff_lambda, diff_subln, swa_sink = f(diff_lambda), f(diff_subln), f(swa_sink)
    w_out, norm_mlp, w_up, w_down, final_norm = f(w_out), f(norm_mlp), f(w_up), f(w_down), f(final_norm)

    if "fused" not in _PROG_CACHE:
        _PROG_CACHE["fused"] = build_program([_lam_init(l) for l in range(DEPTH)])
    nc = _PROG_CACHE["fused"]
    in_maps = []
    c_all = np.ascontiguousarray(c.reshape(NB, 16, 128).transpose(2, 1, 0).reshape(128, 64))
    for core in range(8):
        b, half = core // 2, core % 2
        own = slice(half * 1024, (half + 1) * 1024)
        edge = np.zeros((128, 2), np.float32)
        edge[:, 0] = NEG if half == 0 else 0.0
        edge[:, 1] = NEG if half == 1 else 0.0
        m = {
            "x_own": np.ascontiguousarray(x[b, own]),
            "pos": np.ascontiguousarray(positions[b, own].reshape(NT, 128).T.astype(np.int32)),
            "edge": edge,
            "c_all": c_all,
            "onehot": np.ascontiguousarray(np.repeat((np.arange(4) == b).astype(np.float32)[:, None], 128, axis=1)),
            "final_g": final_norm.reshape(1, D),
        }
        for l in range(DEPTH):
            m.update({
                f"ada_wk{l}": np.ascontiguousarray(
                    ada_w[l].reshape(D, 6, 8, WCOLS)[:, :, core, :].reshape(D, 6 * WCOLS)),
                f"ada_bk{l}": np.ascontiguousarray(
                    ada_b[l].reshape(6, 8, WCOLS)[:, core, :].reshape(1, 6 * WCOLS)),
                f"nmix{l}": _col(norm_mix[l]), f"nmlp{l}": _col(norm_mlp[l]),
                f"w_in{l}": w_in[l], f"w_out{l}": w_out[l],
                f"w_up{l}": w_up[l], f"w_down{l}": w_down[l],
                f"lam{l}": diff_lambda[l].reshape(1, 256),
                f"subln{l}": diff_subln[l].reshape(1, 128),
                f"sink{l}": swa_sink[l].reshape(1, 8),
            })
        in_maps.append(m)
    res = run_bass_kernel_spmd(nc, in_maps, core_ids=list(range(8)))
    out = np.empty((NB, S_LEN, D), np.float32)
    for core in range(8):
        b, half = core // 2, core % 2
        out[b, half * 1024:(half + 1) * 1024] = res.results[core]["y"]
    return out
```
